# Optimizing a Trainium2 kernel written in Bass

```python
import math
import jax, jax.numpy as jnp
from jax import lax
import numpy as np

D_MODEL = 1024
BATCH = 8
SEQ = 8192
DEPTH = 2

N_META = 16
M_HEADS = 4
M_DK = 128
M_DV = 128
M_WIDTH = M_HEADS * M_DV
CONV_W = 4
CHUNK = 64
A_HEADS = 4
A_DK = 64
A_DV = 2 * A_DK
A_WIDTH = A_HEADS * A_DV
Q_BLOCK = 128
D_FF = 2816
ALPHA = (2 * DEPTH) ** 0.25
BETA = (8 * DEPTH) ** -0.25
LN_EPS = 1e-5
NEG = -1e30
PROJ_SPLITS = (M_HEADS * M_DK, M_HEADS * M_DK, M_WIDTH, M_WIDTH, 2 * M_HEADS,
               A_HEADS * 2 * A_DK, A_HEADS * 2 * A_DK, A_WIDTH, 2 * D_MODEL)
D_PROJ = sum(PROJ_SPLITS)

kernel_name = "hybrid_mlstm_diffattn_macaron_deepnorm"


def layer_norm(x, g, b):
    xf = x.astype(jnp.float32)
    mu = jnp.mean(xf, -1, keepdims=True)
    var = jnp.mean(jnp.square(xf - mu), -1, keepdims=True)
    return ((xf - mu) * lax.rsqrt(var + LN_EPS) * g + b).astype(x.dtype)


def swiglu(x, w_gate, w_up, w_down):
    return (jax.nn.silu(x @ w_gate) * (x @ w_up)) @ w_down


def causal_dwconv(u, w, b):
    L = u.shape[1]
    up = jnp.pad(u, ((0, 0), (CONV_W - 1, 0), (0, 0)))
    return b + sum(up[:, j:j + L] * w[j] for j in range(CONV_W))


def alibi_slopes():
    return jnp.array([2.0 ** (-8.0 * (h + 1) / A_HEADS) for h in range(A_HEADS)], jnp.float32)


def mlstm_chunk(carry, inp):
    C, n, m = carry
    q, k, v, log_i, log_f = inp
    T = q.shape[2]
    b = jnp.cumsum(log_f, axis=-1)
    a = b + m[..., None]
    D = b[..., :, None] - b[..., None, :] + log_i[..., None, :]
    causal = jnp.tril(jnp.ones((T, T), dtype=bool))
    D = jnp.where(causal, D, NEG)
    m_t = jnp.maximum(a, jnp.max(D, -1))
    w_state = jnp.exp(a - m_t)
    S = jnp.einsum('bhtk,bhsk->bhts', q, k) * jnp.exp(D - m_t[..., None])
    num = jnp.einsum('bhts,bhsv->bhtv', S, v) + w_state[..., None] * jnp.einsum('bhtk,bhkv->bhtv', q, C)
    den = jnp.sum(S, -1) + w_state * jnp.einsum('bhtk,bhk->bht', q, n)
    h = num / jnp.maximum(jnp.abs(den), jnp.exp(-m_t))[..., None]
    g_end = b[..., -1]
    a_end = g_end + m
    d_end = g_end[..., None] - b + log_i
    m_new = jnp.maximum(a_end, jnp.max(d_end, -1))
    ws = jnp.exp(d_end - m_new[..., None])
    decay = jnp.exp(a_end - m_new)
    C_new = decay[..., None, None] * C + jnp.einsum('bhs,bhsk,bhsv->bhkv', ws, k, v)
    n_new = decay[..., None] * n + jnp.einsum('bhs,bhsk->bhk', ws, k)
    return (C_new, n_new, m_new), h


def mlstm_chunkwise(q, k, v, log_i, log_f):
    B, H, L, _ = q.shape
    nc = (L - N_META) // CHUNK
    carry = (jnp.zeros((B, H, M_DK, M_DV), jnp.float32),
             jnp.zeros((B, H, M_DK), jnp.float32),
             jnp.full((B, H), NEG, jnp.float32))
    ins = (q, k, v, log_i, log_f)
    carry, h_meta = mlstm_chunk(carry, tuple(t[:, :, :N_META] for t in ins))

    def to_chunks(t):
        r = t[:, :, N_META:]
        r = r.reshape((B, H, nc, CHUNK) + r.shape[3:])
        return jnp.moveaxis(r, 2, 0)

    _, h_real = lax.scan(mlstm_chunk, carry, tuple(to_chunks(t) for t in ins))
    h_real = jnp.moveaxis(h_real, 0, 2).reshape(B, H, L - N_META, M_DV)
    return jnp.concatenate([h_meta, h_real], axis=2)


def mlstm_branch(mq, mk, mv, mo, mif, conv_w, conv_b, b_if, norm_g):
    B, L, _ = mq.shape
    qk = jax.nn.silu(causal_dwconv(jnp.concatenate([mq, mk], -1), conv_w, conv_b))
    q, k = jnp.split(qk, 2, axis=-1)

    def heads(t, d):
        return t.reshape(B, L, M_HEADS, d).transpose(0, 2, 1, 3).astype(jnp.float32)

    q = heads(q, M_DK)
    k = heads(k, M_DK) * (M_DK ** -0.5)
    v = heads(mv, M_DV)
    gates = (mif + b_if).astype(jnp.float32).transpose(0, 2, 1)
    log_i = gates[:, :M_HEADS]
    log_f = jax.nn.log_sigmoid(gates[:, M_HEADS:])
    h = mlstm_chunkwise(q, k, v, log_i, log_f)
    mu = jnp.mean(h, -1, keepdims=True)
    var = jnp.mean(jnp.square(h - mu), -1, keepdims=True)
    h = ((h - mu) * lax.rsqrt(var + LN_EPS)).transpose(0, 2, 1, 3).reshape(B, L, M_WIDTH)
    return (h * norm_g * jax.nn.sigmoid(mo.astype(jnp.float32))).astype(mq.dtype)


def diff_attn_branch(aq, ak, av, lam_q1, lam_k1, lam_q2, lam_k2, norm_g, lam_init):
    B, L, _ = aq.shape
    q = aq.reshape(B, L, A_HEADS, 2, A_DK).transpose(0, 2, 3, 1, 4).astype(jnp.float32) * (A_DK ** -0.5)
    k = ak.reshape(B, L, A_HEADS, 2, A_DK).transpose(0, 2, 3, 1, 4).astype(jnp.float32)
    v = av.reshape(B, L, A_HEADS, A_DV).transpose(0, 2, 1, 3).astype(jnp.float32)
    f32 = jnp.float32
    lam = (jnp.exp(jnp.sum(lam_q1.astype(f32) * lam_k1.astype(f32)))
           - jnp.exp(jnp.sum(lam_q2.astype(f32) * lam_k2.astype(f32))) + lam_init)
    slopes = alibi_slopes()

    def attend(qb, tpos, kb, vb):
        spos = jnp.arange(kb.shape[3])
        dist = (tpos[:, None] - spos[None, :]).astype(f32)
        s = jnp.einsum('bhcqd,bhckd->bhcqk', qb, kb) - (slopes[:, None, None] * dist)[None, :, None]
        s = jnp.where(spos[None, :] <= tpos[:, None], s, NEG)
        p = jax.nn.softmax(s, axis=-1)
        return jnp.einsum('bhqk,bhkv->bhqv', p[:, :, 0] - lam * p[:, :, 1], vb)

    o_meta = attend(q[:, :, :, :N_META], jnp.arange(N_META), k[:, :, :, :N_META], v[:, :, :N_META])

    def real_block(j):
        start = N_META + j * Q_BLOCK
        qb = lax.dynamic_slice_in_dim(q, start, Q_BLOCK, axis=3)
        return attend(qb, start + jnp.arange(Q_BLOCK), k, v)

    nb = (L - N_META) // Q_BLOCK
    o_real = lax.map(real_block, jnp.arange(nb))
    o_real = jnp.moveaxis(o_real, 0, 2).reshape(B, A_HEADS, L - N_META, A_DV)
    o = jnp.concatenate([o_meta, o_real], axis=2)
    o = o * lax.rsqrt(jnp.mean(o * o, -1, keepdims=True) + LN_EPS) * (1.0 - lam_init)
    o = o.transpose(0, 2, 1, 3).reshape(B, L, A_WIDTH) * norm_g
    return o.astype(aq.dtype)


def hybrid_mixer(x, w_in, conv_w, conv_b, b_if, m_norm_g, lam_q1, lam_k1, lam_q2, lam_k2,
                 a_norm_g, w_bm, w_ba, b_gate, w_out, lam_init):
    proj = x @ w_in
    offs = np.cumsum(PROJ_SPLITS)[:-1].tolist()
    mq, mk, mv, mo, mif, aq, ak, av, g = jnp.split(proj, offs, axis=-1)
    y_m = mlstm_branch(mq, mk, mv, mo, mif, conv_w, conv_b, b_if, m_norm_g) @ w_bm
    y_a = diff_attn_branch(aq, ak, av, lam_q1, lam_k1, lam_q2, lam_k2, a_norm_g, lam_init) @ w_ba
    g = jax.nn.sigmoid(g + b_gate)
    g_m, g_a = jnp.split(g, 2, axis=-1)
    return (g_m * y_m + g_a * y_a) @ w_out


def setup_inputs(seed: int = 0) -> dict:
    key = jax.random.key(seed)
    ks = jax.random.split(key, 32)
    f32 = jnp.float32

    def nrm(k, shape, scale):
        return jax.random.normal(k, shape, f32) * scale

    Dp = DEPTH
    b_if = jnp.concatenate([
        nrm(ks[10], (Dp, M_HEADS), 0.1),
        jnp.broadcast_to(jnp.linspace(3.0, 6.0, M_HEADS, dtype=f32), (Dp, M_HEADS)) + nrm(ks[11], (Dp, M_HEADS), 0.1),
    ], axis=-1)
    return {
        "x": nrm(ks[0], (BATCH, SEQ, D_MODEL), 1.0),
        "meta": nrm(ks[1], (N_META, D_MODEL), 1.0),
        "ffn1_w_gate": nrm(ks[2], (Dp, D_MODEL, D_FF), D_MODEL ** -0.5),
        "ffn1_w_up": nrm(ks[3], (Dp, D_MODEL, D_FF), D_MODEL ** -0.5),
        "ffn1_w_down": nrm(ks[4], (Dp, D_FF, D_MODEL), BETA * D_FF ** -0.5),
        "ln1_g": 1.0 + nrm(ks[5], (Dp, D_MODEL), 0.02),
        "ln1_b": nrm(ks[6], (Dp, D_MODEL), 0.02),
        "w_in": nrm(ks[7], (Dp, D_MODEL, D_PROJ), D_MODEL ** -0.5),
        "conv_w": nrm(ks[8], (Dp, CONV_W, 2 * M_HEADS * M_DK), CONV_W ** -0.5),
        "conv_b": nrm(ks[9], (Dp, 2 * M_HEADS * M_DK), 0.01),
        "b_if": b_if,
        "m_norm_g": 1.0 + nrm(ks[12], (Dp, M_WIDTH), 0.02),
        "lam_q1": nrm(ks[13], (Dp, A_DK), 0.1),
        "lam_k1": nrm(ks[14], (Dp, A_DK), 0.1),
        "lam_q2": nrm(ks[15], (Dp, A_DK), 0.1),
        "lam_k2": nrm(ks[16], (Dp, A_DK), 0.1),
        "a_norm_g": 1.0 + nrm(ks[17], (Dp, A_WIDTH), 0.02),
        "w_bm": nrm(ks[18], (Dp, M_WIDTH, D_MODEL), BETA * M_WIDTH ** -0.5),
        "w_ba": nrm(ks[19], (Dp, A_WIDTH, D_MODEL), BETA * A_WIDTH ** -0.5),
        "b_gate": nrm(ks[20], (Dp, 2 * D_MODEL), 0.01),
        "w_out": nrm(ks[21], (Dp, D_MODEL, D_MODEL), BETA * D_MODEL ** -0.5),
        "ln2_g": 1.0 + nrm(ks[22], (Dp, D_MODEL), 0.02),
        "ln2_b": nrm(ks[23], (Dp, D_MODEL), 0.02),
        "ffn2_w_gate": nrm(ks[24], (Dp, D_MODEL, D_FF), D_MODEL ** -0.5),
        "ffn2_w_up": nrm(ks[25], (Dp, D_MODEL, D_FF), D_MODEL ** -0.5),
        "ffn2_w_down": nrm(ks[26], (Dp, D_FF, D_MODEL), BETA * D_FF ** -0.5),
        "ln3_g": 1.0 + nrm(ks[27], (Dp, D_MODEL), 0.02),
        "ln3_b": nrm(ks[28], (Dp, D_MODEL), 0.02),
    }


def reference(x, meta, ffn1_w_gate, ffn1_w_up, ffn1_w_down, ln1_g, ln1_b, w_in, conv_w, conv_b,
              b_if, m_norm_g, lam_q1, lam_k1, lam_q2, lam_k2, a_norm_g, w_bm, w_ba, b_gate, w_out,
              ln2_g, ln2_b, ffn2_w_gate, ffn2_w_up, ffn2_w_down, ln3_g, ln3_b):
    B = x.shape[0]
    h = jnp.concatenate([jnp.broadcast_to(meta.astype(x.dtype), (B, N_META, D_MODEL)), x], axis=1)
    for i in range(DEPTH):
        lam_init = 0.8 - 0.6 * math.exp(-0.3 * i)
        h = layer_norm(ALPHA * h + 0.5 * swiglu(h, ffn1_w_gate[i], ffn1_w_up[i], ffn1_w_down[i]), ln1_g[i], ln1_b[i])
        mix = hybrid_mixer(h, w_in[i], conv_w[i], conv_b[i], b_if[i], m_norm_g[i], lam_q1[i], lam_k1[i],
                           lam_q2[i], lam_k2[i], a_norm_g[i], w_bm[i], w_ba[i], b_gate[i], w_out[i], lam_init)
        h = layer_norm(ALPHA * h + mix, ln2_g[i], ln2_b[i])
        h = layer_norm(ALPHA * h + 0.5 * swiglu(h, ffn2_w_gate[i], ffn2_w_up[i], ffn2_w_down[i]), ln3_g[i], ln3_b[i])
    return h[:, N_META:]
```

```python
import math
import numpy as np
import concourse.bass as bass
import concourse.mybir as mybir
from concourse.bass_utils import run_bass_kernel_spmd

AF = mybir.ActivationFunctionType
ALU = mybir.AluOpType
F32 = mybir.dt.float32
BF16 = mybir.dt.bfloat16

D = 1024
FF = 2816
NFF = 22
DP = 5640
NMETA = 16
DEPTH = 2
GT = 512
ALPHA = (2 * DEPTH) ** 0.25
LN_EPS = 1e-5
EPS_RES = LN_EPS / (ALPHA * ALPHA)
C_FFN = 0.5 / ALPHA
C_MIX = 1.0 / ALPHA
SLOPES = [2.0 ** (-8.0 * (h + 1) / 4) for h in range(4)]
WIN_CUT = 60.0
NEGM = -30000.0
LN_KSCALE = math.log(128 ** -0.5)
O_MQ, O_MK, O_MV, O_MO, O_MIF, O_AQ, O_AK, O_AV, O_G = 0, 512, 1024, 1536, 2048, 2056, 2568, 3080, 3592


class T:
    __slots__ = ("ap", "lastw", "readers", "name", "al")

    def __init__(self, ap, name=""):
        self.ap = ap
        self.lastw = None
        self.readers = []
        self.name = name
        self.al = ()

    def __getitem__(self, k):
        return self.ap[k]


class Sched:
    ENGS = ("pe", "act", "dve", "pool", "sp")

    def __init__(self, nc):
        self.nc = nc
        self.ops = {e: [] for e in self.ENGS}
        self.sems = {}
        self.cnt = {}
        for e in self.ENGS:
            self.sems[e] = nc.alloc_semaphore("c_" + e)
            self.cnt[e] = 0
        self.known = {e: {} for e in self.ENGS}
        self.nslot = 0
        self.label = ""
        self.gp = "init"
        self.labels = {e: [] for e in self.ENGS}

    def new_dma_sem(self):
        k = "d%d" % self.nslot
        self.nslot += 1
        self.sems[k] = self.nc.alloc_semaphore(k)
        self.cnt[k] = 0
        return k

    def _deps(self, eng, reads, writes):
        need = {}

        def add(tok):
            k, v = tok
            if need.get(k, 0) < v:
                need[k] = v
        for t in reads:
            if t.lastw is not None:
                add(t.lastw)
            for a in t.al:
                if a.lastw is not None:
                    add(a.lastw)
        for t in writes:
            if t.lastw is not None:
                add(t.lastw)
            for r in t.readers:
                add(r)
            for a in t.al:
                if a.lastw is not None:
                    add(a.lastw)
                for r in a.readers:
                    add(r)
        waits = []
        kn = self.known[eng]
        for k, v in need.items():
            if k == eng and eng == "pe":
                continue
            if kn.get(k, 0) < v:
                kn[k] = v
                waits.append((k, v))
        return waits

    def _post(self, tok, reads, writes):
        for t in reads:
            if len(t.readers) > 24:
                mx = {}
                for k, v in t.readers:
                    if mx.get(k, 0) < v:
                        mx[k] = v
                t.readers = list(mx.items())
            t.readers.append(tok)
        for t in writes:
            t.lastw = tok
            t.readers = []

    def op(self, eng, fn, reads=(), writes=()):
        waits = self._deps(eng, reads, writes)
        self.cnt[eng] += 1
        tok = (eng, self.cnt[eng])
        self.ops[eng].append((waits, fn, eng, 1))
        self.labels[eng].append(self.gp + "/" + self.label)
        self._post(tok, reads, writes)
        return tok

    def dma(self, q, semk, fn, reads=(), writes=()):
        waits = self._deps(q, reads, writes)
        kn = self.known[q]
        if self.cnt[semk] > 0 and kn.get(semk, 0) < self.cnt[semk]:
            kn[semk] = self.cnt[semk]
            waits.append((semk, self.cnt[semk]))
        self.cnt[semk] += 16
        tok = (semk, self.cnt[semk])
        self.ops[q].append((waits, fn, semk, 16))
        self.labels[q].append(self.gp + "/" + self.label)
        self._post(tok, reads, writes)
        return tok

    def wait_all(self, eng, keys):
        kn = self.known[eng]
        waits = []
        for k in keys:
            v = self.cnt[k]
            if v > 0 and kn.get(k, 0) < v:
                kn[k] = v
                waits.append((k, v))
        if waits:
            self.ops[eng].append((waits, None, None, 0))

    def emit(self):
        nc = self.nc
        sems = self.sems
        with nc.Block() as block:
            def runner(name):
                def body(e):
                    for waits, fn, semk, inc in self.ops[name]:
                        for (k, v) in waits:
                            e.wait_ge(sems[k], v)
                        if fn is not None:
                            fn(e).then_inc(sems[semk], inc)
                return body
            block.tensor(runner("pe"))
            block.scalar(runner("act"))
            block.vector(runner("dve"))
            block.gpsimd(runner("pool"))
            block.sync(runner("sp"))


class Rot:
    def __init__(self, items):
        self.items = items
        self.i = 0

    def next(self):
        x = self.items[self.i % len(self.items)]
        self.i += 1
        return x


WNAMES = ["ffn1_w_gate", "ffn1_w_up", "ffn1_w_down", "w_in", "w_bm", "w_ba", "w_out",
          "ffn2_w_gate", "ffn2_w_up", "ffn2_w_down"]
WSHAPES = {"ffn1_w_gate": (D, FF), "ffn1_w_up": (D, FF), "ffn1_w_down": (FF, D), "w_in": (D, DP),
           "w_bm": (512, D), "w_ba": (512, D), "w_out": (D, D),
           "ffn2_w_gate": (D, FF), "ffn2_w_up": (D, FF), "ffn2_w_down": (FF, D)}
BC_NAMES = {"ln1_g": D, "ln1_b": D, "ln2_g": D, "ln2_b": D, "ln3_g": D, "ln3_b": D,
            "m_norm_g": 512, "a_norm_g": 512, "b_if": 8,
            "lam_q1": 64, "lam_k1": 64, "lam_q2": 64, "lam_k2": 64}
NA_M = 68
PPW = 104
DEBUG = False


def host_consts():
    c = {}
    c["ident"] = np.eye(128, dtype=np.float32)
    s = np.arange(128)[:, None]
    t = np.arange(128)[None, :]
    c["umask"] = (s <= t).astype(np.float32)
    c["mneg"] = np.where(s > t, NEGM, 0.0).astype(np.float32)
    p = np.arange(128, dtype=np.float64)[:, None, None]
    sl = np.array(SLOPES, dtype=np.float64)[None, :, None]
    m = np.arange(-3, NA_M - 3, dtype=np.float64)[None, None, :]
    tabA = sl * (p - 128.0 * m - 256.0)
    g = np.arange(16, dtype=np.float64)[None, None, :]
    tabB = sl * (p - 272.0 - 512.0 * g)
    tabC = sl * p * np.ones((1, 1, 1))
    c["alibi"] = np.concatenate([tabA, tabB, tabC], axis=2).reshape(128, -1).astype(np.float32)
    return c


def build_nc(NG):
    L = NMETA + GT * NG
    nc = bass.Bass("TRN2", target_bir_lowering=False)
    S = Sched(nc)

    def din(name, shape, dt=F32):
        return nc.dram_tensor(name, list(shape), dt, kind="ExternalInput").ap()

    x_d = din("x", (GT * NG, D))
    meta_d = din("meta", (NMETA, D))
    w_d = {n: din(n, (DEPTH,) + WSHAPES[n]) for n in WNAMES}
    bc_d = {n: din(n, (DEPTH, BC_NAMES[n])) for n in BC_NAMES}
    pp_d = din("pp", (128, DEPTH * PPW))
    ident_d = din("ident", (128, 128))
    umask_d = din("umask", (128, 128))
    mneg_d = din("mneg", (128, 128))
    NAL = 4 * (NA_M + 16 + 1)
    alibi_d = din("alibi", (128, NAL))
    out_d = nc.dram_tensor("out", [GT * NG, D], F32, kind="ExternalOutput").ap()
    if DEBUG:
        dbg_hn = nc.dram_tensor("dbg_hn", [NMETA + GT, 512], F32, kind="ExternalOutput").ap()
        dbg_ao = nc.dram_tensor("dbg_ao", [NMETA + GT, 512], F32, kind="ExternalOutput").ap()
        dbg_h1 = nc.dram_tensor("dbg_h1", [NMETA + GT, D], F32, kind="ExternalOutput").ap()
    wbf_d = {}
    wbfT = {}
    for l in range(DEPTH):
        for n in WNAMES:
            wbf_d[(l, n)] = nc.dram_tensor("wbf_%d_%s" % (l, n), list(WSHAPES[n]), BF16, kind="Internal").ap()
            wbfT[(l, n)] = [T(wbf_d[(l, n)], "wbf%d" % i_) for i_ in range(4)]
    kt_d = [nc.dram_tensor("ktc%d" % l, [4, 128, L], BF16, kind="Internal").ap() for l in range(DEPTH)]
    v_d = [nc.dram_tensor("vc%d" % l, [4, L, 129], BF16, kind="Internal").ap() for l in range(DEPTH)]
    ktT = [[T(kt_d[l], "ktd") for _ in range(4)] for l in range(DEPTH)]
    vT = [[T(v_d[l], "vd") for _ in range(4)] for l in range(DEPTH)]

    def sb(name, shape, dt):
        return T(nc.alloc_sbuf_tensor("sb_" + name, list(shape), dt).ap(), name)

    def mm(out, lhsT, rhs, start, stop, reads, writes):
        S.op("pe", lambda e: e.matmul(out, lhsT, rhs, start=start, stop=stop), reads, writes)

    def tr(out, in_, idn, reads, writes):
        S.op("pe", lambda e: e.transpose(out, in_, idn), reads, writes)

    def act(out, in_, func, reads, writes, bias=None, scale=None, accum=None):
        kw = {}
        if bias is not None:
            kw["bias"] = bias
        if scale is not None:
            kw["scale"] = scale
        if accum is not None:
            kw["accum_out"] = accum
        S.op("act", lambda e: e.activation(out, in_, func, **kw), reads, writes)

    def tt(eng, out, in0, in1, op, reads, writes):
        S.op(eng, lambda e: e.tensor_tensor(out, in0, in1, op), reads, writes)

    def ts(eng, out, in0, s1, s2, op0, op1, reads, writes):
        if s2 is None:
            S.op(eng, lambda e: e.tensor_scalar(out, in0, s1, None, op0), reads, writes)
        else:
            S.op(eng, lambda e: e.tensor_scalar(out, in0, s1, s2, op0, op1), reads, writes)

    def stt(eng, out, in0, sc, in1, op0, op1, reads, writes):
        S.op(eng, lambda e: e.scalar_tensor_tensor(out, in0, sc, in1, op0, op1), reads, writes)

    def cp(eng, out, in_, reads, writes):
        if eng == "act":
            S.op("act", lambda e: e.activation(out, in_, AF.Copy), reads, writes)
        else:
            S.op(eng, lambda e: e.tensor_copy(out, in_), reads, writes)

    def mset(eng, out, val, writes):
        S.op(eng, lambda e: e.memset(out, val), (), writes)

    def dma(q, semk, out, in_, reads, writes):
        S.dma(q, semk, lambda e: e.dma_start(out=out, in_=in_), reads, writes)

    h = sb("h", [128, 4, D], F32)
    hts = [T(h.ap[:, i_, :], "h%d" % i_) for i_ in range(4)]
    xT = sb("xT", [128, 8, GT], BF16)
    region = nc.alloc_sbuf_tensor("region", [128, 24576], BF16).ap()

    def rview(lo, n, a, name, dt=BF16):
        v = region[:, lo:lo + n]
        if dt == F32:
            v = v.bitcast(F32)
        return T(v.rearrange("p (a b) -> p a b", a=a), name)
    actT = rview(0, 11264, NFF, "actT")
    wD = [rview(11264, 5632, NFF, "wD0"), rview(16896, 5632, NFF, "wD1")]
    qkT = rview(0, 4096, 8, "qkT")
    aqT = rview(4096, 2048, 4, "aqT")
    akT = rview(6144, 2048, 4, "akT")
    hmT = rview(8192, 2048, 4, "hmT")
    haT = rview(10240, 2048, 4, "haT")
    zT = rview(12288, 4096, 8, "zT")
    hn = rview(16384, 4096, 4, "hn", F32)
    ao = rview(20480, 4096, 4, "ao", F32)
    ffn_only = [actT] + wD
    wD = wD + [sb("wD2", [128, NFF, 256], BF16)]
    mix_only = [qkT, aqT, akT, hmT, haT, zT, hn, ao]
    for t_ in ffn_only:
        t_.al = tuple(mix_only)
    for t_ in mix_only:
        t_.al = tuple(ffn_only)
    ftmp = Rot([sb("ftmp%d" % i, [128, GT], F32) for i in range(4)])
    wA = [sb("wA%d" % i, [128, 8, 256], BF16) for i in range(4)]
    wA_sem = [S.new_dma_sem() for _ in wA]
    wArot = Rot(list(range(len(wA))))
    wD_sem = [S.new_dma_sem() for _ in wD]
    wDrot = Rot(list(range(len(wD))))
    wB = [sb("wB%d" % i, [128, 8, 512], BF16) for i in range(3)]
    wB_sem = [S.new_dma_sem() for _ in wB]
    wBrot = Rot(list(range(len(wB))))
    wmif = sb("wmif", [128, 8, 8], BF16)
    wmif_sem = S.new_dma_sem()
    lnp = [sb("lnp%d" % i, [128, 2, D], F32) for i in range(1)]
    lnpg = [T(t_.ap[:, 0, :], "lnpg") for t_ in lnp]
    lnpb = [T(t_.ap[:, 1, :], "lnpb") for t_ in lnp]
    lnp_sem = [(S.new_dma_sem(), S.new_dma_sem()) for _ in lnp]
    lnprot = Rot(list(range(len(lnp))))
    raw = Rot([sb("raw%d" % i, [128, 3 + GT], F32) for i in range(2)])
    cacc = Rot([sb("cacc%d" % i, [128, GT], F32) for i in range(2)])
    vaug = sb("vaug", [128, 4, 4, 129], BF16)
    KTc = [sb("KTc%d" % i, [128, 2048], BF16) for i in range(2)]
    Vc = [sb("Vc%d" % i, [128, 16, 129], BF16) for i in range(2)]
    KV_sem = [(S.new_dma_sem(), S.new_dma_sem()) for _ in KTc]
    KVrot = Rot(list(range(len(KTc))))
    PT = Rot([sb("PT%d" % i, [128, GT], BF16) for i in range(4)])
    vw = sb("vw", [128, 4, 129], BF16)
    ktok = sb("ktok", [128, 512], BF16)
    SM = sb("SM", [128, 4, 128], BF16)
    Cst = [sb("Cst%d" % l, [128, 4, 129], F32) for l in range(DEPTH)]
    Cbf = [sb("Cbf%d" % l, [128, 4, 129], BF16) for l in range(DEPTH)]
    ctmp = sb("ctmp", [128, 4, 129], F32)
    hist = [sb("hist%d" % l, [128, 8, 3], F32) for l in range(DEPTH)]
    metaKT = [sb("metaKT%d" % l, [128, 4, 16], BF16) for l in range(DEPTH)]
    metaV = [sb("metaV%d" % l, [128, 4, 129], BF16) for l in range(DEPTH)]
    sm = Rot([sb("sm%d" % i, [128, 64], F32) for i in range(3)])
    stats = sb("stats", [128, 8, 6], F32)
    gsm = sb("gsm", [128, 4, 32], F32)
    sgm = sb("sgm", [128, 512], F32)
    hmtmp = sb("hmtmp", [128, 512], F32)
    hbf = sb("hbf", [128, 512], BF16)
    ssq = sb("ssq", [128, 4, 4], F32)
    habf = sb("habf", [128, 4, 128], BF16)
    rr = sb("rr", [128, 8], F32)
    ident = sb("ident", [128, 128], F32)
    identb = sb("identb", [128, 128], BF16)
    umaskf = sb("umaskf", [128, 128], F32)
    umaskb = sb("umaskb", [128, 128], BF16)
    mnegf = sb("mnegf", [128, 128], F32)
    mnegb = sb("mnegb", [128, 128], BF16)
    onesf = sb("onesf", [128, 128], F32)
    alibi = sb("alibi", [128, NAL], F32)
    pp = sb("pp", [128, DEPTH * PPW], F32)
    mng = [sb("mng%d" % l, [128, 512], F32) for l in range(DEPTH)]
    ang = [sb("ang%d" % l, [128, 512], F32) for l in range(DEPTH)]
    bif = [sb("bif%d" % l, [128, 8], F32) for l in range(DEPTH)]
    lamv = sb("lamv", [128, 4, 64], F32)
    lamt = [sb("lamt%d" % l, [128, 4], F32) for l in range(DEPTH)]
    cst = sb("cst", [128, 4], F32)
    print("sbuf bytes remaining/partition:", nc.sbuf_bytes_remaining if hasattr(nc, "sbuf_bytes_remaining") else "?")

    banks = [T(nc.alloc_psum_tensor("bank%d" % i, [128, 512], F32).ap(), "bank%d" % i) for i in range(8)]
    mmrot = Rot(banks[0:4])
    tprot = Rot(banks[4:6])

    csem = S.new_dma_sem()
    dma("sp", csem, ident.ap, ident_d, (), [ident])
    dma("sp", csem, umaskf.ap, umask_d, (), [umaskf])
    dma("sp", csem, mnegf.ap, mneg_d, (), [mnegf])
    dma("sp", csem, alibi.ap, alibi_d, (), [alibi])
    dma("sp", csem, pp.ap, pp_d, (), [pp])
    for l in range(DEPTH):
        dma("sp", csem, mng[l].ap, bc_d["m_norm_g"][l:l + 1, :].partition_broadcast(128), (), [mng[l]])
        dma("sp", csem, ang[l].ap, bc_d["a_norm_g"][l:l + 1, :].partition_broadcast(128), (), [ang[l]])
        dma("sp", csem, bif[l].ap, bc_d["b_if"][l:l + 1, :].partition_broadcast(128), (), [bif[l]])
    cp("dve", identb.ap, ident.ap, [ident], [identb])
    cp("dve", umaskb.ap, umaskf.ap, [umaskf], [umaskb])
    cp("dve", mnegb.ap, mnegf.ap, [mnegf], [mnegb])
    mset("pool", onesf.ap, 1.0, [onesf])
    mset("pool", cst.ap[:, 0:1], EPS_RES, [cst])
    mset("pool", cst.ap[:, 1:2], LN_EPS, [cst])
    mset("pool", cst.ap[:, 2:3], LN_KSCALE, [cst])
    mset("pool", cst.ap[:, 3:4], 1.0, [cst])
    mset("pool", vaug.ap, 1.0, [vaug])
    for l in range(DEPTH):
        mset("pool", Cst[l].ap, 0.0, [Cst[l]])
        mset("pool", Cbf[l].ap, 0.0, [Cbf[l]])
        mset("pool", hist[l].ap, 0.0, [hist[l]])
        mset("pool", metaV[l].ap, 1.0, [metaV[l]])
    for l in range(DEPTH):
        lam_init = 0.8 - 0.6 * math.exp(-0.3 * l)
        for j, n in enumerate(["lam_q1", "lam_k1", "lam_q2", "lam_k2"]):
            dma("sp", csem, lamv.ap[:, j, :], bc_d[n][l:l + 1, :].partition_broadcast(128), (), [lamv])
        s0 = sm.next()
        tt("dve", s0.ap[:, 0:64], lamv.ap[:, 0, :], lamv.ap[:, 1, :], ALU.mult, [lamv], [s0])
        s1 = sm.next()
        S.op("dve", lambda e, s0=s0, s1=s1: e.reduce_sum(s1.ap[:, 0:1], s0.ap[:, 0:64], mybir.AxisListType.X), [s0], [s1])
        tt("dve", s0.ap[:, 0:64], lamv.ap[:, 2, :], lamv.ap[:, 3, :], ALU.mult, [lamv, s1], [s0])
        S.op("dve", lambda e, s0=s0, s1=s1: e.reduce_sum(s1.ap[:, 1:2], s0.ap[:, 0:64], mybir.AxisListType.X), [s0], [s1])
        act(s1.ap[:, 2:4], s1.ap[:, 0:2], AF.Exp, [s1], [s1])
        tt("dve", s1.ap[:, 4:5], s1.ap[:, 3:4], s1.ap[:, 2:3], ALU.subtract, [s1], [s1])
        ts("dve", lamt[l].ap[:, 0:1], s1.ap[:, 4:5], -lam_init, None, ALU.add, None, [s1], [lamt[l]])
        ts("dve", ang[l].ap, ang[l].ap, 1.0 - lam_init, None, ALU.mult, None, [ang[l]], [ang[l]])

    wlane = [S.new_dma_sem() for _ in range(4)]
    npiece = 0
    for l in range(DEPTH):
        for n in WNAMES:
            rows = WSHAPES[n][0]
            step = 128
            for r0 in range(0, rows, step):
                r1 = min(rows, r0 + step)
                ln_ = npiece % 4
                npiece += 1
                dma("pool", wlane[ln_], wbf_d[(l, n)][r0:r1, :], w_d[n][l, r0:r1, :], (), [wbfT[(l, n)][ln_]])

    groups = [dict(meta=True, nt=[NMETA], pos0=0, row0=None, g=-1)]
    for g in range(NG):
        groups.append(dict(meta=False, nt=[128] * 4, pos0=NMETA + GT * g, row0=GT * g, g=g))
    xsem = S.new_dma_sem()
    osem = [S.new_dma_sem() for _ in range(4)]
    ksem = [[S.new_dma_sem() for _ in range(4)] for _ in range(DEPTH)]
    vsem = [[S.new_dma_sem() for _ in range(4)] for _ in range(DEPTH)]

    def load_wA(l, n, c0, ncols=256):
        i = wArot.next()
        src = wbf_d[(l, n)][:, c0:c0 + ncols].rearrange("(kc p) c -> p kc c", p=128)
        dma("sp", wA_sem[i], wA[i].ap[:, :, 0:ncols], src, wbfT[(l, n)], [wA[i]])
        return wA[i]

    def load_wB(l, n, c0, ncols, nkc):
        i = wBrot.next()
        src = wbf_d[(l, n)][:, c0:c0 + ncols].rearrange("(kc p) c -> p kc c", p=128)
        if ncols == 512:
            dst = wB[i].ap[:, 0:nkc, :]
        else:
            dst = wB[i].ap.rearrange("p a b -> p (a b)")[:, 0:nkc * ncols].rearrange("p (a b) -> p a b", a=nkc)
        dma("sp", wB_sem[i], dst, src, wbfT[(l, n)], [wB[i]])
        return wB[i], dst

    def stage_load(G):
        if G["meta"]:
            dma("sp", xsem, h.ap[0:NMETA, 0, :], meta_d, (), hts)
        else:
            src = x_d[G["row0"]:G["row0"] + GT, :].rearrange("(i p) d -> p i d", p=128)
            dma("sp", xsem, h.ap, src, (), hts)

    stg = [(cacc.items[0], cacc.items[1]), (raw.items[0], raw.items[1])]
    stg_sem = [(S.new_dma_sem(), S.new_dma_sem()), (S.new_dma_sem(), S.new_dma_sem())]

    def prefetch_xT(Gn):
        S.label = "load"
        for ti, nt in enumerate(Gn["nt"]):
            r0 = Gn["row0"] + ti * 128
            sl = ti % 2
            for half in range(2):
                tb_ = stg[sl][half]
                dma("sp", stg_sem[sl][half], tb_.ap[0:nt, 0:512], x_d[r0:r0 + nt, half * 512:(half + 1) * 512], (), [tb_])
                pt = tprot.next()
                for c in range(4):
                    tr(pt.ap[:, c * 128:c * 128 + nt], tb_.ap[0:nt, c * 128:(c + 1) * 128], ident.ap[0:nt, 0:nt], [tb_, ident], [pt])
                src = pt.ap.rearrange("p (c t) -> p c t", t=128)[:, :, 0:nt]
                cp("act", xT.ap[:, half * 4:half * 4 + 4, ti * 128:ti * 128 + nt], src, [pt], [xT])

    def stage_ffn(G, l, which, prefetch=None):
        S.label = "ffn_gu"
        NT = sum(G["nt"])
        ntl = G["nt"]
        pre = "ffn%d_" % which
        for blk in range(11):
            wg = load_wA(l, pre + "w_gate", blk * 256)
            wu = load_wA(l, pre + "w_up", blk * 256)
            for cc in range(2):
                ch = blk * 2 + cc
                pg = mmrot.next()
                pu = mmrot.next()
                for kc in range(8):
                    mm(pg.ap[:, 0:NT], wg.ap[:, kc, cc * 128:(cc + 1) * 128], xT.ap[:, kc, 0:NT], kc == 0, kc == 7, [wg, xT], [pg])
                for kc in range(8):
                    mm(pu.ap[:, 0:NT], wu.ap[:, kc, cc * 128:(cc + 1) * 128], xT.ap[:, kc, 0:NT], kc == 0, kc == 7, [wu, xT], [pu])
                sg = ftmp.next()
                act(sg.ap[:, 0:NT], pg.ap[:, 0:NT], AF.Silu, [pg], [sg])
                tt("dve", actT.ap[:, ch, 0:NT], sg.ap[:, 0:NT], pu.ap[:, 0:NT], ALU.mult, [sg, pu], [actT])
        if prefetch is not None:
            prefetch_xT(prefetch)
        S.label = "ffn_down"
        nti = len(ntl)
        nt0 = ntl[0]

        def tail(dch, ys):
            pt = tprot.next()
            for ti, nt in enumerate(ntl):
                tr(pt.ap[0:nt, ti * 128:(ti + 1) * 128], ys.ap[:, ti * 128:ti * 128 + nt], ident.ap, [ys, ident], [pt])
            hv = h.ap[0:nt0, 0:nti, dch * 128:(dch + 1) * 128]
            pv = pt.ap[0:nt0, 0:nti * 128].rearrange("p (i c) -> p i c", c=128)
            stt("dve", hv, pv, C_FFN, hv, ALU.mult, ALU.add, [pt] + hts[0:nti], hts[0:nti])
        pending = None
        for blk in range(4):
            i = wDrot.next()
            src = wbf_d[(l, pre + "w_down")][:, blk * 256:(blk + 1) * 256].rearrange("(fc p) c -> p fc c", p=128)
            dma("sp", wD_sem[i], wD[i].ap, src, wbfT[(l, pre + "w_down")], [wD[i]])
            for cc in range(2):
                dch = blk * 2 + cc
                py = mmrot.next()
                for fc in range(NFF):
                    mm(py.ap[:, 0:NT], wD[i].ap[:, fc, cc * 128:(cc + 1) * 128], actT.ap[:, fc, 0:NT], fc == 0, fc == NFF - 1, [wD[i], actT], [py])
                ys = ftmp.next()
                cp("act", ys.ap[:, 0:NT], py.ap[:, 0:NT], [py], [ys])
                if pending is not None:
                    tail(*pending)
                pending = (dch, ys)
        tail(*pending)

    def stage_ln(G, l, k, final):
        S.label = "ln"
        ntl = G["nt"]
        gname, bname = ["ln1_g", "ln2_g", "ln3_g"][k], ["ln1_b", "ln2_b", "ln3_b"][k]
        pg0 = l * PPW + 56 + (2 * k) * 8
        pb0 = l * PPW + 56 + (2 * k + 1) * 8
        i = lnprot.next()
        dma("sp", lnp_sem[i][0], lnp[i].ap[:, 0, :], bc_d[gname][l:l + 1, :].partition_broadcast(128), (), [lnpg[i]])
        dma("sp", lnp_sem[i][1], lnp[i].ap[:, 1, :], bc_d[bname][l:l + 1, :].partition_broadcast(128), (), [lnpb[i]])
        for ti, nt in enumerate(ntl):
            hv = h.ap[0:nt, ti, :]
            S.op("dve", lambda e, nt=nt, ti=ti: e.bn_stats(stats.ap[0:nt, 0, :], h.ap[0:nt, ti, 0:512]), [hts[ti]], [stats])
            S.op("dve", lambda e, nt=nt, ti=ti: e.bn_stats(stats.ap[0:nt, 1, :], h.ap[0:nt, ti, 512:1024]), [hts[ti]], [stats])
            s0 = sm.next()
            S.op("dve", lambda e, nt=nt, s0=s0: e.bn_aggr(s0.ap[0:nt, 0:2], stats.ap[0:nt, 0:2, :]), [stats], [s0])
            act(s0.ap[0:nt, 2:3], s0.ap[0:nt, 1:2], AF.Ln, [s0, cst], [s0], bias=cst.ap[0:nt, 0:1], scale=1.0)
            act(s0.ap[0:nt, 3:4], s0.ap[0:nt, 2:3], AF.Exp, [s0], [s0], scale=-0.5)
            ts("dve", hv, hv, s0.ap[0:nt, 0:1], s0.ap[0:nt, 3:4], ALU.subtract, ALU.mult, [hts[ti], s0], [hts[ti]])
            if not final:
                for half in range(2):
                    pt = tprot.next()
                    for c in range(4):
                        ch = half * 4 + c
                        tr(pt.ap[:, c * 128:c * 128 + nt], h.ap[0:nt, ti, ch * 128:(ch + 1) * 128], ident.ap[0:nt, 0:nt], [hts[ti], ident], [pt])
                    for c in range(4):
                        ch = half * 4 + c
                        if c != 3:
                            act(xT.ap[:, ch, ti * 128:ti * 128 + nt], pt.ap[:, c * 128:c * 128 + nt], AF.Identity, [pt, pp], [xT],
                                bias=pp.ap[:, pb0 + ch:pb0 + ch + 1], scale=pp.ap[:, pg0 + ch:pg0 + ch + 1])
                        else:
                            ts("dve", xT.ap[:, ch, ti * 128:ti * 128 + nt], pt.ap[:, c * 128:c * 128 + nt],
                               pp.ap[:, pg0 + ch:pg0 + ch + 1], pp.ap[:, pb0 + ch:pb0 + ch + 1], ALU.mult, ALU.add, [pt, pp], [xT])
            tt("pool", hv, hv, lnp[i].ap[0:nt, 0, :], ALU.mult, [hts[ti], lnpg[i]], [hts[ti]])
            tt("pool", hv, hv, lnp[i].ap[0:nt, 1, :], ALU.add, [hts[ti], lnpb[i]], [hts[ti]])
            if final and not G["meta"]:
                r0 = G["row0"] + ti * 128
                dma("pool", osem[ti], out_d[r0:r0 + nt, :], hv, [hts[ti]], [])

    def stage_proj_feat(G, l):
        S.label = "proj_feat"
        NT = sum(G["nt"])
        ppb = l * PPW
        for blk in range(4):
            w = load_wA(l, "w_in", O_MQ + blk * 256)
            for cc in range(2):
                ch = blk * 2 + cc
                p = mmrot.next()
                for kc in range(8):
                    mm(p.ap[:, 0:NT], w.ap[:, kc, cc * 128:(cc + 1) * 128], xT.ap[:, kc, 0:NT], kc == 0, kc == 7, [w, xT], [p])
                r = raw.next()
                wcol = lambda j: pp.ap[:, ppb + ch * 4 + j:ppb + ch * 4 + j + 1]
                cp("dve", r.ap[:, 0:3], hist[l].ap[:, ch, :], [hist[l]], [r])
                cp("act", r.ap[:, 3:3 + NT], p.ap[:, 0:NT], [p], [r])
                ca = cacc.next()
                act(ca.ap[:, 0:NT], p.ap[:, 0:NT], AF.Identity, [p, pp], [ca], scale=wcol(3))
                for j in range(3):
                    stt("dve", ca.ap[:, 0:NT], r.ap[:, j:j + NT], wcol(j), ca.ap[:, 0:NT], ALU.mult, ALU.add, [r, pp, ca], [ca])
                cp("dve", hist[l].ap[:, ch, :], r.ap[:, NT:NT + 3], [r], [hist[l]])
                act(qkT.ap[:, ch, 0:NT], ca.ap[:, 0:NT], AF.Silu, [ca, pp], [qkT], bias=pp.ap[:, ppb + 32 + ch:ppb + 32 + ch + 1], scale=1.0)
        for blk in range(4):
            w = load_wA(l, "w_in", O_AQ + blk * 256)
            for cc in range(2):
                ch = blk * 2 + cc
                p = mmrot.next()
                for kc in range(8):
                    mm(p.ap[:, 0:NT], w.ap[:, kc, cc * 128:(cc + 1) * 128], xT.ap[:, kc, 0:NT], kc == 0, kc == 7, [w, xT], [p])
                if ch < 4:
                    cp("dve", aqT.ap[:, ch, 0:NT], p.ap[:, 0:NT], [p], [aqT])
                else:
                    cp("act", akT.ap[:, ch - 4, 0:NT], p.ap[:, 0:NT], [p], [akT])
        if G["meta"]:
            cp("pool", metaKT[l].ap, akT.ap[:, :, 0:NMETA], [akT], [metaKT[l]])
        else:
            for hh in range(4):
                dma("pool", ksem[l][hh], kt_d[l][hh, :, G["pos0"]:G["pos0"] + NT], akT.ap[:, hh, 0:NT], [akT], [ktT[l][hh]])

    def stage_mlstm(G, l):
        S.label = "mlstm"
        ntl = G["nt"]
        nti = len(ntl)
        nt0 = ntl[0]
        wmv, wmv_v = load_wB(l, "w_in", O_MV, 512, 8)
        wav, wav_v = load_wB(l, "w_in", O_AV, 512, 8)
        src = wbf_d[(l, "w_in")][:, O_MIF:O_MIF + 8].rearrange("(kc p) c -> p kc c", p=128)
        dma("sp", wmif_sem, wmif.ap, src, wbfT[(l, "w_in")], [wmif])
        bS, bV, bK, bO0, bO1, bC0, bC1, bA = banks[6], banks[0], banks[1], banks[2], banks[3], banks[4], banks[5], banks[7]
        for ti, nt in enumerate(ntl):
            c0 = ti * 128
            for kc in range(8):
                mm(bS.ap[0:nt, ti * 16:ti * 16 + 8], xT.ap[:, kc, c0:c0 + nt], wmif.ap[:, kc, :], kc == 0, kc == 7, [xT, wmif], [bS])
        bS3 = bS.ap[0:nt0, 0:nti * 16].rearrange("p (t c) -> p t c", c=16)
        bS3f = bS.ap[:, 0:nti * 16].rearrange("p (t c) -> p t c", c=16)
        g3 = gsm.ap[0:nt0, 0:nti, :]
        tt("dve", g3[:, :, 0:8], bS3[:, :, 0:8], bif[l].ap[0:nt0, :].unsqueeze(1).to_broadcast([nt0, nti, 8]), ALU.add, [bS, bif[l]], [gsm])
        act(g3[:, :, 8:12], g3[:, :, 4:8], AF.Exp, [gsm], [gsm], scale=-1.0)
        act(g3[:, :, 12:16], g3[:, :, 8:12], AF.Ln, [gsm, cst], [gsm], bias=cst.ap[0:nt0, 3:4], scale=1.0)
        S.label = "mlstm_f32"
        for ti, nt in enumerate(ntl):
            mm(bS.ap[0:nt, ti * 16 + 8:ti * 16 + 12], umaskf.ap[0:nt, 0:nt], gsm.ap[0:nt, ti, 12:16], True, True, [umaskf, gsm], [bS])
            mm(bS.ap[:, ti * 16 + 12:ti * 16 + 16], onesf.ap[0:nt, :], gsm.ap[0:nt, ti, 12:16], True, True, [onesf, gsm], [bS])
        S.label = "mlstm"
        tt("dve", g3[:, :, 16:20], g3[:, :, 0:4], bS3[:, :, 8:12], ALU.add, [gsm, bS], [gsm])
        act(g3[:, :, 20:24], g3[:, :, 16:20], AF.Exp, [gsm, cst], [gsm], bias=cst.ap[0:nt0, 2:3], scale=1.0)
        act(g3[:, :, 24:28], bS3[:, :, 8:12], AF.Exp, [bS], [gsm])
        act(gsm.ap[:, 0:nti, 28:32], bS3f[:, :, 12:16], AF.Exp, [bS], [gsm], scale=-1.0)
        for ti, nt in enumerate(ntl):
            c0 = ti * 128
            ew = gsm.ap[0:nt, ti, 20:24]
            emb = gsm.ap[0:nt, ti, 24:28]
            tt("dve", ctmp.ap, Cst[l].ap, gsm.ap[:, ti, 28:32].unsqueeze(2).to_broadcast([128, 4, 129]), ALU.mult, [Cst[l], gsm], [ctmp])
            for kc in range(8):
                mm(bA.ap[0:nt, :], xT.ap[:, kc, c0:c0 + nt], wav_v[:, kc, :], kc == 0, kc == 7, [xT, wav], [bA])
            cp("act", vaug.ap[0:nt, ti, :, 0:128], bA.ap[0:nt, :].rearrange("p (h c) -> p h c", c=128), [bA], [vaug])
            if G["meta"]:
                cp("pool", metaV[l].ap[0:nt, :, 0:128], vaug.ap[0:nt, ti, :, 0:128], [vaug], [metaV[l]])
            else:
                p0 = G["pos0"] + c0
                dma("pool", vsem[l][ti], v_d[l][:, p0:p0 + nt, :].rearrange("h t c -> t h c"), vaug.ap[0:nt, ti, :, :], [vaug], [vT[l][ti]])
            for kc in range(8):
                mm(bV.ap[0:nt, :], xT.ap[:, kc, c0:c0 + nt], wmv_v[:, kc, :], kc == 0, kc == 7, [xT, wmv], [bV])
            tt("dve", vw.ap[0:nt, :, 0:128], bV.ap[0:nt, :].rearrange("p (h c) -> p h c", c=128),
               ew.unsqueeze(2).to_broadcast([nt, 4, 128]), ALU.mult, [bV, gsm], [vw])
            cp("dve", vw.ap[0:nt, :, 128:129], ew.unsqueeze(2), [gsm], [vw])
            bKb = bK.ap.bitcast(BF16)
            for hh in range(4):
                tr(bKb[0:nt, hh * 128:(hh + 1) * 128], qkT.ap[:, 4 + hh, c0:c0 + nt], identb.ap, [qkT, identb], [bK])
            cp("act", ktok.ap[0:nt, :], bKb[0:nt, 0:512], [bK], [ktok])
            for hh in range(4):
                mm(banks[7].ap[0:nt, hh * 128:hh * 128 + nt], qkT.ap[:, 4 + hh, c0:c0 + nt], qkT.ap[:, hh, c0:c0 + nt], True, True, [qkT], [banks[7]])
            tt("dve", SM.ap[0:nt, :, 0:nt], banks[7].ap[0:nt, :].rearrange("p (h c) -> p h c", c=128)[:, :, 0:nt],
               umaskb.ap[0:nt, 0:nt].unsqueeze(1).to_broadcast([nt, 4, nt]), ALU.mult, [banks[7], umaskb], [SM])
            for hh in range(4):
                bo = bO0 if hh < 2 else bO1
                o0 = (hh % 2) * 129
                mm(bo.ap[0:nt, o0:o0 + 129], SM.ap[0:nt, hh, 0:nt], vw.ap[0:nt, hh, :], True, False, [SM, vw], [bo])
                mm(bo.ap[0:nt, o0:o0 + 129], qkT.ap[:, hh, c0:c0 + nt], Cbf[l].ap[:, hh, :], False, True, [qkT, Cbf[l]], [bo])
            for hh in range(4):
                bc = bC0 if hh < 2 else bC1
                o0 = (hh % 2) * 129
                mm(bc.ap[:, o0:o0 + 129], ktok.ap[0:nt, hh * 128:(hh + 1) * 128], vw.ap[0:nt, hh, :], True, True, [ktok, vw], [bc])
            for hh in range(4):
                bc = bC0 if hh < 2 else bC1
                o0 = (hh % 2) * 129
                stt("dve", Cst[l].ap[:, hh, :], bc.ap[:, o0:o0 + 129], gsm.ap[:, ti, 28 + hh:29 + hh], ctmp.ap[:, hh, :], ALU.mult, ALU.add, [bc, gsm, ctmp], [Cst[l]])
            cp("act", Cbf[l].ap, Cst[l].ap, [Cst[l]], [Cbf[l]])
            s1 = sm.next()
            for half, bo in enumerate((bO0, bO1)):
                cp("dve", s1.ap[0:nt, half * 2:half * 2 + 2], bo.ap[0:nt, 0:258].rearrange("p (h c) -> p h c", c=129)[:, :, 128], [bo], [s1])
            stt("dve", s1.ap[0:nt, 4:8], s1.ap[0:nt, 0:4], -1.0, s1.ap[0:nt, 0:4], ALU.mult, ALU.max, [s1], [s1])
            tt("dve", s1.ap[0:nt, 8:12], s1.ap[0:nt, 4:8], emb, ALU.max, [s1, gsm], [s1])
            tt("dve", s1.ap[0:nt, 12:16], s1.ap[0:nt, 8:12], s1.ap[0:nt, 8:12], ALU.mult, [s1], [s1])
            for hh in range(4):
                bo = bO0 if hh < 2 else bO1
                o0 = (hh % 2) * 129
                S.op("dve", lambda e, nt=nt, hh=hh, bo=bo, o0=o0: e.bn_stats(stats.ap[0:nt, hh, :], bo.ap[0:nt, o0:o0 + 128]), [bo], [stats])
            for hh in range(4):
                S.op("dve", lambda e, nt=nt, hh=hh, s1=s1: e.bn_aggr(s1.ap[0:nt, 32 + 2 * hh:34 + 2 * hh], stats.ap[0:nt, hh:hh + 1, :]), [stats], [s1])
            varv = s1.ap[0:nt, 32:40].rearrange("p (h c) -> p h c", c=2)[:, :, 1]
            stt("dve", s1.ap[0:nt, 16:20], s1.ap[0:nt, 12:16], LN_EPS, varv, ALU.mult, ALU.add, [s1], [s1])
            act(s1.ap[0:nt, 20:24], s1.ap[0:nt, 16:20], AF.Ln, [s1], [s1])
            act(s1.ap[0:nt, 24:28], s1.ap[0:nt, 20:24], AF.Exp, [s1], [s1], scale=-0.5)
            for hh in range(4):
                bo = bO0 if hh < 2 else bO1
                o0 = (hh % 2) * 129
                ts("dve", hn.ap[0:nt, ti, hh * 128:(hh + 1) * 128], bo.ap[0:nt, o0:o0 + 128],
                   s1.ap[0:nt, 32 + 2 * hh:33 + 2 * hh], s1.ap[0:nt, 24 + hh:25 + hh], ALU.subtract, ALU.mult, [bo, s1], [hn])

    def stage_out_m(G, l):
        S.label = "out_a"
        ntl = G["nt"]
        wmo, wmo_v = load_wB(l, "w_in", O_MO, 512, 8)
        for ti, nt in enumerate(ntl):
            c0 = ti * 128
            pm = mmrot.next()
            for kc in range(8):
                mm(pm.ap[0:nt, :], xT.ap[:, kc, c0:c0 + nt], wmo_v[:, kc, :], kc == 0, kc == 7, [xT, wmo], [pm])
            act(sgm.ap[0:nt, :], pm.ap[0:nt, :], AF.Sigmoid, [pm], [sgm])
            tt("dve", hmtmp.ap[0:nt, :], hn.ap[0:nt, ti, :], sgm.ap[0:nt, :], ALU.mult, [hn, sgm], [hmtmp])
            tt("dve", hbf.ap[0:nt, :], hmtmp.ap[0:nt, :], mng[l].ap[0:nt, :], ALU.mult, [hmtmp, mng[l]], [hbf])
            pt = tprot.next()
            ptb = pt.ap.bitcast(BF16)
            for fc in range(4):
                tr(ptb[:, fc * 128:fc * 128 + nt], hbf.ap[0:nt, fc * 128:(fc + 1) * 128], identb.ap[0:nt, 0:nt], [hbf, identb], [pt])
            cp("act", hmT.ap[:, :, c0:c0 + nt], ptb[:, 0:512].rearrange("p (c t) -> p c t", t=128)[:, :, 0:nt], [pt], [hmT])

    attn_tail_cb = [None]

    def stage_attn(G, l):
        S.label = "attn"
        ntl = G["nt"]
        NT = sum(ntl)
        nti = len(ntl)
        pos0 = G["pos0"]
        g = G["g"]
        sbanks = Rot([(banks[0], banks[1]), (banks[2], banks[3])])
        def accv(qt, c):
            a_ = qt * 2 + c
            return banks[4 + a_ // 3], (a_ % 3) * 129
        tpb = banks[7]
        mset("dve", ssq.ap, 0.0, [ssq])
        deferred = [None]

        def head_tail1(hh):
            S.label = "attn_fin"
            for qt in range(nti):
                ntq = ntl[qt]
                s0 = sm.next()
                act(s0.ap[0:ntq, 0:1], ssq.ap[0:ntq, qt, hh:hh + 1], AF.Ln, [ssq, cst], [s0], bias=cst.ap[0:ntq, 1:2], scale=1.0 / 128.0)
                act(s0.ap[0:ntq, 1:2], s0.ap[0:ntq, 0:1], AF.Exp, [s0], [s0], scale=-0.5)
                stt("dve", habf.ap[0:ntq, qt, :], ao.ap[0:ntq, qt, hh * 128:(hh + 1) * 128], s0.ap[0:ntq, 1:2],
                    ang[l].ap[0:ntq, hh * 128:(hh + 1) * 128], ALU.mult, ALU.mult, [ao, s0, ang[l]], [habf])

        def head_tail2(hh):
            S.label = "attn_tail"
            tb = tpb.ap.bitcast(BF16)
            for qt in range(nti):
                ntq = ntl[qt]
                tr(tb[:, qt * 128:qt * 128 + ntq], habf.ap[0:ntq, qt, :], identb.ap[0:ntq, 0:ntq], [habf, identb], [tpb])
            cp("act", haT.ap[:, hh, 0:NT], tb[:, 0:NT], [tpb], [haT])

        for hh in range(4):
            slope = SLOPES[hh]
            bank_started = {}
            started = [[False, False] for _ in range(nti)]
            units = []
            if not G["meta"]:
                if slope * (pos0 - NMETA + 1) <= WIN_CUT:
                    colB = hh * (NA_M + 17) + NA_M + g
                    units.append(("sb", metaKT[l].ap[:, hh, :], metaV[l].ap[0:NMETA, hh, :], NMETA, colB, 0, [metaKT[l], metaV[l]], None))
                nprev = 4 * g
                u_lo = 0
                while u_lo < nprev and slope * (pos0 - (NMETA + 128 * (u_lo + 1)) + 1) > WIN_CUT:
                    u_lo += 1
                u = u_lo
                while u < nprev:
                    n_u = min(16, nprev - u)
                    for j in range(n_u):
                        m = nprev - (u + j)
                        colA = hh * (NA_M + 17) + (m + 3)
                        units.append(("dram", u, n_u, j, colA))
                    u += n_u
            for kt in range(nti):
                nk = ntl[kt]
                if G["meta"]:
                    colA = hh * (NA_M + 17) + NA_M + 16
                else:
                    colA = hh * (NA_M + 17) + (-kt + 3)
                units.append(("sb", akT.ap[:, hh, kt * 128:kt * 128 + nk], vaug.ap[0:nk, kt, hh, :], nk, colA, kt * 128, [akT, vaug], kt))

            cur_chunk = [None]

            def resolve(un):
                if un[0] == "sb":
                    return un[1:]
                _, u0, n_u, j, colA = un
                if cur_chunk[0] is None or cur_chunk[0][0] != u0:
                    bi = KVrot.next()
                    p_lo = NMETA + 128 * u0
                    dma("sp", KV_sem[bi][0], KTc[bi].ap[:, 0:128 * n_u], kt_d[l][hh, :, p_lo:p_lo + 128 * n_u], [ktT[l][hh]], [KTc[bi]])
                    dma("sp", KV_sem[bi][1], Vc[bi].ap[:, 0:n_u, :], v_d[l][hh, p_lo:p_lo + 128 * n_u, :].rearrange("(u p) c -> p u c", p=128), vT[l], [Vc[bi]])
                    cur_chunk[0] = (u0, bi)
                bi = cur_chunk[0][1]
                return (KTc[bi].ap[:, j * 128:(j + 1) * 128], Vc[bi].ap[:, j, :], 128, colA, 0, [KTc[bi], Vc[bi]], None)

            def phaseA(un):
                S.label = "attn_S"
                kT_ap, v_ap, nk, colA, q0, kdeps, diag_kt = resolve(un)
                sp = sbanks.next()
                pts = []
                for c in range(2):
                    bk = sp[c]
                    rows = slice(64 * c, 64 * c + 64)
                    if diag_kt is None:
                        mm(bk.ap[0:nk, q0:NT], kT_ap[rows, :], aqT.ap[rows, hh, q0:NT], True, True, kdeps + [aqT], [bk])
                    else:
                        ntq = ntl[diag_kt]
                        mm(bk.ap[0:nk, q0:q0 + ntq], kT_ap[rows, :], aqT.ap[rows, hh, q0:q0 + ntq], True, False, kdeps + [aqT], [bk])
                        mm(bk.ap[0:nk, q0:q0 + ntq], identb.ap[0:nk, 0:nk], mnegb.ap[0:nk, 0:ntq], False, True, [identb, mnegb], [bk])
                        if q0 + 128 < NT:
                            mm(bk.ap[0:nk, q0 + 128:NT], kT_ap[rows, :], aqT.ap[rows, hh, q0 + 128:NT], True, True, kdeps + [aqT], [bk])
                    pt = PT.next()
                    act(pt.ap[0:nk, q0:NT], bk.ap[0:nk, q0:NT], AF.Exp, [bk, alibi], [pt], bias=alibi.ap[0:nk, colA:colA + 1], scale=0.125)
                    pts.append(pt)
                return (pts, v_ap, nk, q0, kdeps, diag_kt)

            def phaseB(st):
                S.label = "attn_PV"
                pts, v_ap, nk, q0, kdeps, diag_kt = st
                for c in range(2):
                    pt = pts[c]
                    for qt in range(q0 // 128, nti):
                        ntq = ntl[qt]
                        last = (diag_kt is not None and qt == diag_kt)
                        ab, col = accv(qt, c)
                        first = ab.name not in bank_started
                        bank_started[ab.name] = True
                        S.op("pe", lambda e, o_=ab.ap[0:ntq, col:col + 129], l_=pt.ap[0:nk, qt * 128:qt * 128 + ntq], r_=v_ap, f_=first, s_=last:
                             e.matmul(o_, l_, r_, start=f_, stop=s_, skip_group_check=True), kdeps + [pt], [ab])

            prev = None
            defer_at = min(3, len(units) - 1)
            for ui, un in enumerate(units):
                st = phaseA(un)
                if ui == defer_at and deferred[0] is not None:
                    head_tail2(deferred[0])
                    deferred[0] = None
                if prev is not None:
                    phaseB(prev)
                prev = st
            phaseB(prev)
            S.label = "attn_fin"
            for qt in range(nti):
                ntq = ntl[qt]
                ab0, c0_ = accv(qt, 0)
                ab1, c1_ = accv(qt, 1)
                S.op("dve", lambda e, ntq=ntq, ab0=ab0, c0_=c0_: e.reciprocal(rr.ap[0:ntq, 0:1], ab0.ap[0:ntq, c0_ + 128:c0_ + 129]), [ab0], [rr])
                S.op("dve", lambda e, ntq=ntq, ab1=ab1, c1_=c1_: e.reciprocal(rr.ap[0:ntq, 1:2], ab1.ap[0:ntq, c1_ + 128:c1_ + 129]), [ab1], [rr])
                tt("dve", rr.ap[0:ntq, 2:3], rr.ap[0:ntq, 1:2], lamt[l].ap[0:ntq, 0:1], ALU.mult, [rr, lamt[l]], [rr])
                ov = ao.ap[0:ntq, qt, hh * 128:(hh + 1) * 128]
                ts("dve", ov, ab0.ap[0:ntq, c0_:c0_ + 128], rr.ap[0:ntq, 0:1], None, ALU.mult, None, [ab0, rr], [ao])
                stt("dve", ov, ab1.ap[0:ntq, c1_:c1_ + 128], rr.ap[0:ntq, 2:3], ov, ALU.mult, ALU.add, [ab1, rr, ao], [ao])
                sq = ftmp.next()
                act(sq.ap[0:ntq, 0:128], ov, AF.Square, [ao], [sq, ssq], accum=ssq.ap[0:ntq, qt, hh:hh + 1])
            head_tail1(hh)
            deferred[0] = hh
        attn_tail_cb[0] = (lambda hh_=deferred[0]: head_tail2(hh_))

    def stage_out(G, l):
        ntl = G["nt"]
        NT = sum(ntl)
        ppb = l * PPW
        S.label = "out_b"
        wbm, wbm_v = load_wB(l, "w_bm", 0, 1024, 4)
        wba, wba_v = load_wB(l, "w_ba", 0, 1024, 4)
        for j in range(4):
            wgm = load_wA(l, "w_in", O_G + j * 256)
            wga = load_wA(l, "w_in", O_G + 1024 + j * 256)
            for cc in range(2):
                c = 2 * j + cc
                pgm, pga, pym, pya = mmrot.next(), mmrot.next(), mmrot.next(), mmrot.next()
                for kc in range(8):
                    mm(pgm.ap[:, 0:NT], wgm.ap[:, kc, cc * 128:(cc + 1) * 128], xT.ap[:, kc, 0:NT], kc == 0, kc == 7, [wgm, xT], [pgm])
                for kc in range(8):
                    mm(pga.ap[:, 0:NT], wga.ap[:, kc, cc * 128:(cc + 1) * 128], xT.ap[:, kc, 0:NT], kc == 0, kc == 7, [wga, xT], [pga])
                for fc in range(4):
                    mm(pym.ap[:, 0:NT], wbm_v[:, fc, c * 128:(c + 1) * 128], hmT.ap[:, fc, 0:NT], fc == 0, fc == 3, [wbm, hmT], [pym])
                if attn_tail_cb[0] is not None:
                    attn_tail_cb[0]()
                    attn_tail_cb[0] = None
                    S.label = "out_b"
                for fc in range(4):
                    mm(pya.ap[:, 0:NT], wba_v[:, fc, c * 128:(c + 1) * 128], haT.ap[:, fc, 0:NT], fc == 0, fc == 3, [wba, haT], [pya])
                s_m = ftmp.next()
                s_a = ftmp.next()
                act(s_m.ap[:, 0:NT], pgm.ap[:, 0:NT], AF.Sigmoid, [pgm, pp], [s_m], bias=pp.ap[:, ppb + 40 + c:ppb + 41 + c], scale=1.0)
                act(s_a.ap[:, 0:NT], pga.ap[:, 0:NT], AF.Sigmoid, [pga, pp], [s_a], bias=pp.ap[:, ppb + 48 + c:ppb + 49 + c], scale=1.0)
                tt("dve", s_m.ap[:, 0:NT], s_m.ap[:, 0:NT], pym.ap[:, 0:NT], ALU.mult, [s_m, pym], [s_m])
                tt("dve", s_a.ap[:, 0:NT], s_a.ap[:, 0:NT], pya.ap[:, 0:NT], ALU.mult, [s_a, pya], [s_a])
                tt("dve", zT.ap[:, c, 0:NT], s_m.ap[:, 0:NT], s_a.ap[:, 0:NT], ALU.add, [s_m, s_a], [zT])
        S.label = "out_c"
        wo = []
        for half in range(2):
            wo.append(load_wB(l, "w_out", half * 512, 512, 8))
        for ti, nt in enumerate(ntl):
            c0 = ti * 128
            for half in range(2):
                pm = mmrot.next()
                for kc in range(8):
                    mm(pm.ap[0:nt, :], zT.ap[:, kc, c0:c0 + nt], wo[half][1][:, kc, :], kc == 0, kc == 7, [zT, wo[half][0]], [pm])
                hv = h.ap[0:nt, ti, half * 512:(half + 1) * 512]
                stt("dve", hv, pm.ap[0:nt, :], C_MIX, hv, ALU.mult, ALU.add, [pm, hts[ti]], [hts[ti]])

    def initial_xT(G):
        ntl = G["nt"]
        for ti, nt in enumerate(ntl):
            for half in range(2):
                pt = tprot.next()
                for c in range(4):
                    ch = half * 4 + c
                    tr(pt.ap[:, c * 128:c * 128 + nt], h.ap[0:nt, ti, ch * 128:(ch + 1) * 128], ident.ap[0:nt, 0:nt], [hts[ti], ident], [pt])
                src = pt.ap.rearrange("p (c t) -> p c t", t=128)[:, :, 0:nt]
                cp("act", xT.ap[:, half * 4:half * 4 + 4, ti * 128:ti * 128 + nt], src, [pt], [xT])

    for gi, G in enumerate(groups):
        S.gp = "g%d" % G["g"]
        S.label = "load"
        stage_load(G)
        if gi == 0:
            initial_xT(G)
        Gn = groups[gi + 1] if gi + 1 < len(groups) else None
        for l in range(DEPTH):
            stage_ffn(G, l, 1)
            stage_ln(G, l, 0, False)
            stage_proj_feat(G, l)
            stage_mlstm(G, l)
            stage_out_m(G, l)
            stage_attn(G, l)
            stage_out(G, l)
            stage_ln(G, l, 1, False)
            stage_ffn(G, l, 2, prefetch=(Gn if l == DEPTH - 1 else None))
            stage_ln(G, l, 2, final=(l == DEPTH - 1))
    S.wait_all("pool", osem)
    S.wait_all("sp", osem)
    print("instr counts:", {k: v for k, v in S.cnt.items() if k in Sched.ENGS}, "sems:", len(S.sems))
    S.emit()
    nc._labels = S.labels
    return nc


def make_in_maps(inputs, NG, ncores):
    consts = host_consts()
    x = np.asarray(inputs["x"], dtype=np.float32)
    pp = np.zeros((128, DEPTH, PPW), dtype=np.float32)
    cw = np.asarray(inputs["conv_w"], dtype=np.float32)
    cb = np.asarray(inputs["conv_b"], dtype=np.float32)
    bg = np.asarray(inputs["b_gate"], dtype=np.float32)
    pp[:, :, 0:32] = cw.reshape(DEPTH, 4, 8, 128).transpose(3, 0, 2, 1).reshape(128, DEPTH, 32)
    pp[:, :, 32:40] = cb.reshape(DEPTH, 8, 128).transpose(2, 0, 1)
    pp[:, :, 40:56] = bg.reshape(DEPTH, 16, 128).transpose(2, 0, 1)
    for k_, n_ in enumerate(["ln1_g", "ln1_b", "ln2_g", "ln2_b", "ln3_g", "ln3_b"]):
        v_ = np.asarray(inputs[n_], dtype=np.float32)
        pp[:, :, 56 + k_ * 8:56 + (k_ + 1) * 8] = v_.reshape(DEPTH, 8, 128).transpose(2, 0, 1)
    pp = np.ascontiguousarray(pp.reshape(128, DEPTH * PPW))
    shared = {n: np.ascontiguousarray(np.asarray(inputs[n], dtype=np.float32)) for n in WNAMES}
    for n in BC_NAMES:
        shared[n] = np.ascontiguousarray(np.asarray(inputs[n], dtype=np.float32))
    shared["meta"] = np.ascontiguousarray(np.asarray(inputs["meta"], dtype=np.float32))
    shared["pp"] = pp
    shared.update(consts)
    maps = []
    for c in range(ncores):
        m = dict(shared)
        m["x"] = np.ascontiguousarray(x[c, :GT * NG, :])
        maps.append(m)
    return maps


_NC_CACHE = {}


def kernel(**inputs):
    NG = 16
    ncores = 8
    if NG not in _NC_CACHE:
        _NC_CACHE[NG] = build_nc(NG)
    nc = _NC_CACHE[NG]
    maps = make_in_maps(inputs, NG, ncores)
    res = run_bass_kernel_spmd(nc, maps, core_ids=list(range(ncores)))
    out = np.stack([np.asarray(r["out"], dtype=np.float32) for r in res.results], axis=0)
    return out
```

```python
import math
import numpy as np
import concourse.bass as bass
import concourse.mybir as mybir
from concourse.bass_utils import run_bass_kernel_spmd

AF = mybir.ActivationFunctionType
ALU = mybir.AluOpType
F32 = mybir.dt.float32
BF16 = mybir.dt.bfloat16

D = 1024
FF = 2816
NFF = 22
DP = 5640
NMETA = 16
DEPTH = 2
GT = 512
ALPHA = (2 * DEPTH) ** 0.25
LN_EPS = 1e-5
EPS_RES = LN_EPS / (ALPHA * ALPHA)
C_FFN = 0.5 / ALPHA
C_MIX = 1.0 / ALPHA
SLOPES = [2.0 ** (-8.0 * (h + 1) / 4) for h in range(4)]
WIN_CUT = 60.0
NEGM = -30000.0
LN_KSCALE = math.log(128 ** -0.5)
O_MQ, O_MK, O_MV, O_MO, O_MIF, O_AQ, O_AK, O_AV, O_G = 0, 512, 1024, 1536, 2048, 2056, 2568, 3080, 3592


class T:
    __slots__ = ("ap", "lastw", "readers", "name", "al")

    def __init__(self, ap, name=""):
        self.ap = ap
        self.lastw = None
        self.readers = []
        self.name = name
        self.al = ()

    def __getitem__(self, k):
        return self.ap[k]


class Sched:
    ENGS = ("pe", "act", "dve", "pool", "sp")

    def __init__(self, nc):
        self.nc = nc
        self.ops = {e: [] for e in self.ENGS}
        self.sems = {}
        self.cnt = {}
        for e in self.ENGS:
            self.sems[e] = nc.alloc_semaphore("c_" + e)
            self.cnt[e] = 0
        self.known = {e: {} for e in self.ENGS}
        self.nslot = 0
        self.label = ""
        self.gp = "init"
        self.labels = {e: [] for e in self.ENGS}

    def new_dma_sem(self):
        k = "d%d" % self.nslot
        self.nslot += 1
        self.sems[k] = self.nc.alloc_semaphore(k)
        self.cnt[k] = 0
        return k

    def _deps(self, eng, reads, writes):
        need = {}

        def add(tok):
            k, v = tok
            if need.get(k, 0) < v:
                need[k] = v
        for t in reads:
            if t.lastw is not None:
                add(t.lastw)
            for a in t.al:
                if a.lastw is not None:
                    add(a.lastw)
        for t in writes:
            if t.lastw is not None:
                add(t.lastw)
            for r in t.readers:
                add(r)
            for a in t.al:
                if a.lastw is not None:
                    add(a.lastw)
                for r in a.readers:
                    add(r)
        waits = []
        kn = self.known[eng]
        for k, v in need.items():
            if k == eng and eng == "pe":
                continue
            if kn.get(k, 0) < v:
                kn[k] = v
                waits.append((k, v))
        return waits

    def _post(self, tok, reads, writes):
        for t in reads:
            if len(t.readers) > 24:
                mx = {}
                for k, v in t.readers:
                    if mx.get(k, 0) < v:
                        mx[k] = v
                t.readers = list(mx.items())
            t.readers.append(tok)
        for t in writes:
            t.lastw = tok
            t.readers = []

    def op(self, eng, fn, reads=(), writes=()):
        waits = self._deps(eng, reads, writes)
        self.cnt[eng] += 1
        tok = (eng, self.cnt[eng])
        self.ops[eng].append((waits, fn, eng, 1))
        self.labels[eng].append(self.gp + "/" + self.label)
        self._post(tok, reads, writes)
        return tok

    def dma(self, q, semk, fn, reads=(), writes=()):
        waits = self._deps(q, reads, writes)
        kn = self.known[q]
        if self.cnt[semk] > 0 and kn.get(semk, 0) < self.cnt[semk]:
            kn[semk] = self.cnt[semk]
            waits.append((semk, self.cnt[semk]))
        self.cnt[semk] += 16
        tok = (semk, self.cnt[semk])
        self.ops[q].append((waits, fn, semk, 16))
        self.labels[q].append(self.gp + "/" + self.label)
        self._post(tok, reads, writes)
        return tok

    def wait_all(self, eng, keys):
        kn = self.known[eng]
        waits = []
        for k in keys:
            v = self.cnt[k]
            if v > 0 and kn.get(k, 0) < v:
                kn[k] = v
                waits.append((k, v))
        if waits:
            self.ops[eng].append((waits, None, None, 0))

    def emit(self):
        nc = self.nc
        sems = self.sems
        with nc.Block() as block:
            def runner(name):
                def body(e):
                    for waits, fn, semk, inc in self.ops[name]:
                        for (k, v) in waits:
                            e.wait_ge(sems[k], v)
                        if fn is not None:
                            fn(e).then_inc(sems[semk], inc)
                return body
            block.tensor(runner("pe"))
            block.scalar(runner("act"))
            block.vector(runner("dve"))
            block.gpsimd(runner("pool"))
            block.sync(runner("sp"))


class Rot:
    def __init__(self, items):
        self.items = items
        self.i = 0

    def next(self):
        x = self.items[self.i % len(self.items)]
        self.i += 1
        return x


WNAMES = ["ffn1_w_gate", "ffn1_w_up", "ffn1_w_down", "w_in", "w_bm", "w_ba", "w_out",
          "ffn2_w_gate", "ffn2_w_up", "ffn2_w_down"]
WSHAPES = {"ffn1_w_gate": (D, FF), "ffn1_w_up": (D, FF), "ffn1_w_down": (FF, D), "w_in": (D, DP),
           "w_bm": (512, D), "w_ba": (512, D), "w_out": (D, D),
           "ffn2_w_gate": (D, FF), "ffn2_w_up": (D, FF), "ffn2_w_down": (FF, D)}
BC_NAMES = {"ln1_g": D, "ln1_b": D, "ln2_g": D, "ln2_b": D, "ln3_g": D, "ln3_b": D,
            "m_norm_g": 512, "a_norm_g": 512, "b_if": 8,
            "lam_q1": 64, "lam_k1": 64, "lam_q2": 64, "lam_k2": 64}
NA_M = 68
PPW = 104
DEBUG = False


def host_consts():
    c = {}
    c["ident"] = np.eye(128, dtype=np.float32)
    s = np.arange(128)[:, None]
    t = np.arange(128)[None, :]
    c["umask"] = (s <= t).astype(np.float32)
    c["mneg"] = np.where(s > t, NEGM, 0.0).astype(np.float32)
    p = np.arange(128, dtype=np.float64)[:, None, None]
    sl = np.array(SLOPES, dtype=np.float64)[None, :, None]
    m = np.arange(-3, NA_M - 3, dtype=np.float64)[None, None, :]
    tabA = sl * (p - 128.0 * m - 256.0)
    g = np.arange(16, dtype=np.float64)[None, None, :]
    tabB = sl * (p - 272.0 - 512.0 * g)
    tabC = sl * p * np.ones((1, 1, 1))
    c["alibi"] = np.concatenate([tabA, tabB, tabC], axis=2).reshape(128, -1).astype(np.float32)
    return c


def build_nc(NG):
    L = NMETA + GT * NG
    nc = bass.Bass("TRN2", target_bir_lowering=False)
    S = Sched(nc)

    def din(name, shape, dt=F32):
        return nc.dram_tensor(name, list(shape), dt, kind="ExternalInput").ap()

    x_d = din("x", (GT * NG, D))
    meta_d = din("meta", (NMETA, D))
    w_d = {n: din(n, (DEPTH,) + WSHAPES[n]) for n in WNAMES}
    bc_d = {n: din(n, (DEPTH, BC_NAMES[n])) for n in BC_NAMES}
    pp_d = din("pp", (128, DEPTH * PPW))
    ident_d = din("ident", (128, 128))
    umask_d = din("umask", (128, 128))
    mneg_d = din("mneg", (128, 128))
    NAL = 4 * (NA_M + 16 + 1)
    alibi_d = din("alibi", (128, NAL))
    out_d = nc.dram_tensor("out", [GT * NG, D], F32, kind="ExternalOutput").ap()
    if DEBUG:
        dbg_hn = nc.dram_tensor("dbg_hn", [NMETA + GT, 512], F32, kind="ExternalOutput").ap()
        dbg_ao = nc.dram_tensor("dbg_ao", [NMETA + GT, 512], F32, kind="ExternalOutput").ap()
        dbg_h1 = nc.dram_tensor("dbg_h1", [NMETA + GT, D], F32, kind="ExternalOutput").ap()
    wbf_d = {}
    wbfT = {}
    for l in range(DEPTH):
        for n in WNAMES:
            wbf_d[(l, n)] = nc.dram_tensor("wbf_%d_%s" % (l, n), list(WSHAPES[n]), BF16, kind="Internal").ap()
            wbfT[(l, n)] = [T(wbf_d[(l, n)], "wbf%d" % i_) for i_ in range(4)]
    kt_d = [nc.dram_tensor("ktc%d" % l, [4, 128, L], BF16, kind="Internal").ap() for l in range(DEPTH)]
    v_d = [nc.dram_tensor("vc%d" % l, [4, L, 129], BF16, kind="Internal").ap() for l in range(DEPTH)]
    ktT = [[T(kt_d[l], "ktd") for _ in range(4)] for l in range(DEPTH)]
    vT = [[T(v_d[l], "vd") for _ in range(4)] for l in range(DEPTH)]

    def sb(name, shape, dt):
        return T(nc.alloc_sbuf_tensor("sb_" + name, list(shape), dt).ap(), name)

    def mm(out, lhsT, rhs, start, stop, reads, writes):
        S.op("pe", lambda e: e.matmul(out, lhsT, rhs, start=start, stop=stop), reads, writes)

    def tr(out, in_, idn, reads, writes):
        S.op("pe", lambda e: e.transpose(out, in_, idn), reads, writes)

    def act(out, in_, func, reads, writes, bias=None, scale=None, accum=None):
        kw = {}
        if bias is not None:
            kw["bias"] = bias
        if scale is not None:
            kw["scale"] = scale
        if accum is not None:
            kw["accum_out"] = accum
        S.op("act", lambda e: e.activation(out, in_, func, **kw), reads, writes)

    def tt(eng, out, in0, in1, op, reads, writes):
        S.op(eng, lambda e: e.tensor_tensor(out, in0, in1, op), reads, writes)

    def ts(eng, out, in0, s1, s2, op0, op1, reads, writes):
        if s2 is None:
            S.op(eng, lambda e: e.tensor_scalar(out, in0, s1, None, op0), reads, writes)
        else:
            S.op(eng, lambda e: e.tensor_scalar(out, in0, s1, s2, op0, op1), reads, writes)

    def stt(eng, out, in0, sc, in1, op0, op1, reads, writes):
        S.op(eng, lambda e: e.scalar_tensor_tensor(out, in0, sc, in1, op0, op1), reads, writes)

    def cp(eng, out, in_, reads, writes):
        if eng == "act":
            S.op("act", lambda e: e.activation(out, in_, AF.Copy), reads, writes)
        else:
            S.op(eng, lambda e: e.tensor_copy(out, in_), reads, writes)

    def mset(eng, out, val, writes):
        S.op(eng, lambda e: e.memset(out, val), (), writes)

    def dma(q, semk, out, in_, reads, writes):
        S.dma(q, semk, lambda e: e.dma_start(out=out, in_=in_), reads, writes)

    h = sb("h", [128, 4, D], F32)
    hts = [T(h.ap[:, i_, :], "h%d" % i_) for i_ in range(4)]
    xT = sb("xT", [128, 8, GT], BF16)
    region = nc.alloc_sbuf_tensor("region", [128, 24576], BF16).ap()

    def rview(lo, n, a, name, dt=BF16):
        v = region[:, lo:lo + n]
        if dt == F32:
            v = v.bitcast(F32)
        return T(v.rearrange("p (a b) -> p a b", a=a), name)
    actT = rview(0, 11264, NFF, "actT")
    wD = [rview(11264, 5632, NFF, "wD0"), rview(16896, 5632, NFF, "wD1")]
    qkT = rview(0, 4096, 8, "qkT")
    aqT = rview(4096, 2048, 4, "aqT")
    akT = rview(6144, 2048, 4, "akT")
    hmT = rview(8192, 2048, 4, "hmT")
    haT = rview(10240, 2048, 4, "haT")
    zT = rview(12288, 4096, 8, "zT")
    hn = rview(16384, 4096, 4, "hn", F32)
    ao = rview(20480, 4096, 4, "ao", F32)
    ffn_only = [actT] + wD
    wD = wD + [sb("wD2", [128, NFF, 256], BF16)]
    mix_only = [qkT, aqT, akT, hmT, haT, zT, hn, ao]
    for t_ in ffn_only:
        t_.al = tuple(mix_only)
    for t_ in mix_only:
        t_.al = tuple(ffn_only)
    ftmp = Rot([sb("ftmp%d" % i, [128, GT], F32) for i in range(4)])
    wA = [sb("wA%d" % i, [128, 8, 256], BF16) for i in range(4)]
    wA_sem = [S.new_dma_sem() for _ in wA]
    wArot = Rot(list(range(len(wA))))
    wD_sem = [S.new_dma_sem() for _ in wD]
    wDrot = Rot(list(range(len(wD))))
    wB = [sb("wB%d" % i, [128, 8, 512], BF16) for i in range(3)]
    wB_sem = [S.new_dma_sem() for _ in wB]
    wBrot = Rot(list(range(len(wB))))
    wmif = sb("wmif", [128, 8, 8], BF16)
    wmif_sem = S.new_dma_sem()
    lnp = [sb("lnp%d" % i, [128, 2, D], F32) for i in range(1)]
    lnpg = [T(t_.ap[:, 0, :], "lnpg") for t_ in lnp]
    lnpb = [T(t_.ap[:, 1, :], "lnpb") for t_ in lnp]
    lnp_sem = [(S.new_dma_sem(), S.new_dma_sem()) for _ in lnp]
    lnprot = Rot(list(range(len(lnp))))
    raw = Rot([sb("raw%d" % i, [128, 3 + GT], F32) for i in range(2)])
    cacc = Rot([sb("cacc%d" % i, [128, GT], F32) for i in range(2)])
    vaug = sb("vaug", [128, 4, 4, 129], BF16)
    KTc = [sb("KTc%d" % i, [128, 2048], BF16) for i in range(2)]
    Vc = [sb("Vc%d" % i, [128, 16, 129], BF16) for i in range(2)]
    KV_sem = [(S.new_dma_sem(), S.new_dma_sem()) for _ in KTc]
    KVrot = Rot(list(range(len(KTc))))
    PT = Rot([sb("PT%d" % i, [128, GT], BF16) for i in range(4)])
    vw = sb("vw", [128, 4, 129], BF16)
    ktok = sb("ktok", [128, 512], BF16)
    SM = sb("SM", [128, 4, 128], BF16)
    Cst = [sb("Cst%d" % l, [128, 4, 129], F32) for l in range(DEPTH)]
    Cbf = [sb("Cbf%d" % l, [128, 4, 129], BF16) for l in range(DEPTH)]
    ctmp = sb("ctmp", [128, 4, 129], F32)
    hist = [sb("hist%d" % l, [128, 8, 3], F32) for l in range(DEPTH)]
    metaKT = [sb("metaKT%d" % l, [128, 4, 16], BF16) for l in range(DEPTH)]
    metaV = [sb("metaV%d" % l, [128, 4, 129], BF16) for l in range(DEPTH)]
    sm = Rot([sb("sm%d" % i, [128, 64], F32) for i in range(3)])
    stats = sb("stats", [128, 8, 6], F32)
    gsm = sb("gsm", [128, 4, 32], F32)
    sgm = sb("sgm", [128, 512], F32)
    hmtmp = sb("hmtmp", [128, 512], F32)
    hbf = sb("hbf", [128, 512], BF16)
    hbf1 = sb("hbf1", [128, 512], BF16)
    ssq = sb("ssq", [128, 4, 4], F32)
    habf = sb("habf", [128, 4, 128], BF16)
    rr = sb("rr", [128, 8], F32)
    ident = sb("ident", [128, 128], F32)
    identb = sb("identb", [128, 128], BF16)
    umaskf = sb("umaskf", [128, 128], F32)
    umaskb = sb("umaskb", [128, 128], BF16)
    mnegf = sb("mnegf", [128, 128], F32)
    mnegb = sb("mnegb", [128, 128], BF16)
    onesf = sb("onesf", [128, 128], F32)
    alibi = sb("alibi", [128, NAL], F32)
    pp = sb("pp", [128, DEPTH * PPW], F32)
    mng = [sb("mng%d" % l, [128, 512], F32) for l in range(DEPTH)]
    ang = [sb("ang%d" % l, [128, 512], F32) for l in range(DEPTH)]
    bif = [sb("bif%d" % l, [128, 8], F32) for l in range(DEPTH)]
    lamv = sb("lamv", [128, 4, 64], F32)
    lamt = [sb("lamt%d" % l, [128, 4], F32) for l in range(DEPTH)]
    cst = sb("cst", [128, 4], F32)
    print("sbuf bytes remaining/partition:", nc.sbuf_bytes_remaining if hasattr(nc, "sbuf_bytes_remaining") else "?")

    banks = [T(nc.alloc_psum_tensor("bank%d" % i, [128, 512], F32).ap(), "bank%d" % i) for i in range(8)]
    mmrot = Rot(banks[0:4])
    tprot = Rot(banks[4:6])

    csem = S.new_dma_sem()
    dma("sp", csem, ident.ap, ident_d, (), [ident])
    dma("sp", csem, umaskf.ap, umask_d, (), [umaskf])
    dma("sp", csem, mnegf.ap, mneg_d, (), [mnegf])
    dma("sp", csem, alibi.ap, alibi_d, (), [alibi])
    dma("sp", csem, pp.ap, pp_d, (), [pp])
    for l in range(DEPTH):
        dma("sp", csem, mng[l].ap, bc_d["m_norm_g"][l:l + 1, :].partition_broadcast(128), (), [mng[l]])
        dma("sp", csem, ang[l].ap, bc_d["a_norm_g"][l:l + 1, :].partition_broadcast(128), (), [ang[l]])
        dma("sp", csem, bif[l].ap, bc_d["b_if"][l:l + 1, :].partition_broadcast(128), (), [bif[l]])
    cp("dve", identb.ap, ident.ap, [ident], [identb])
    cp("dve", umaskb.ap, umaskf.ap, [umaskf], [umaskb])
    cp("dve", mnegb.ap, mnegf.ap, [mnegf], [mnegb])
    mset("pool", onesf.ap, 1.0, [onesf])
    mset("pool", cst.ap[:, 0:1], EPS_RES, [cst])
    mset("pool", cst.ap[:, 1:2], LN_EPS, [cst])
    mset("pool", cst.ap[:, 2:3], LN_KSCALE, [cst])
    mset("pool", cst.ap[:, 3:4], 1.0, [cst])
    mset("pool", vaug.ap, 1.0, [vaug])
    for l in range(DEPTH):
        mset("pool", Cst[l].ap, 0.0, [Cst[l]])
        mset("pool", Cbf[l].ap, 0.0, [Cbf[l]])
        mset("pool", hist[l].ap, 0.0, [hist[l]])
        mset("pool", metaV[l].ap, 1.0, [metaV[l]])
    for l in range(DEPTH):
        lam_init = 0.8 - 0.6 * math.exp(-0.3 * l)
        for j, n in enumerate(["lam_q1", "lam_k1", "lam_q2", "lam_k2"]):
            dma("sp", csem, lamv.ap[:, j, :], bc_d[n][l:l + 1, :].partition_broadcast(128), (), [lamv])
        s0 = sm.next()
        tt("dve", s0.ap[:, 0:64], lamv.ap[:, 0, :], lamv.ap[:, 1, :], ALU.mult, [lamv], [s0])
        s1 = sm.next()
        S.op("dve", lambda e, s0=s0, s1=s1: e.reduce_sum(s1.ap[:, 0:1], s0.ap[:, 0:64], mybir.AxisListType.X), [s0], [s1])
        tt("dve", s0.ap[:, 0:64], lamv.ap[:, 2, :], lamv.ap[:, 3, :], ALU.mult, [lamv, s1], [s0])
        S.op("dve", lambda e, s0=s0, s1=s1: e.reduce_sum(s1.ap[:, 1:2], s0.ap[:, 0:64], mybir.AxisListType.X), [s0], [s1])
        act(s1.ap[:, 2:4], s1.ap[:, 0:2], AF.Exp, [s1], [s1])
        tt("dve", s1.ap[:, 4:5], s1.ap[:, 3:4], s1.ap[:, 2:3], ALU.subtract, [s1], [s1])
        ts("dve", lamt[l].ap[:, 0:1], s1.ap[:, 4:5], -lam_init, None, ALU.add, None, [s1], [lamt[l]])
        ts("dve", ang[l].ap, ang[l].ap, 1.0 - lam_init, None, ALU.mult, None, [ang[l]], [ang[l]])

    wlane = [S.new_dma_sem() for _ in range(4)]
    npiece = 0
    for l in range(DEPTH):
        for n in WNAMES:
            rows = WSHAPES[n][0]
            step = 128
            for r0 in range(0, rows, step):
                r1 = min(rows, r0 + step)
                ln_ = npiece % 4
                npiece += 1
                dma("pool", wlane[ln_], wbf_d[(l, n)][r0:r1, :], w_d[n][l, r0:r1, :], (), [wbfT[(l, n)][ln_]])

    groups = [dict(meta=True, nt=[NMETA], pos0=0, row0=None, g=-1)]
    for g in range(NG):
        groups.append(dict(meta=False, nt=[128] * 4, pos0=NMETA + GT * g, row0=GT * g, g=g))
    xsem = S.new_dma_sem()
    osem = [S.new_dma_sem() for _ in range(4)]
    ksem = [[S.new_dma_sem() for _ in range(4)] for _ in range(DEPTH)]
    vsem = [[S.new_dma_sem() for _ in range(4)] for _ in range(DEPTH)]

    def load_wA(l, n, c0, ncols=256):
        i = wArot.next()
        src = wbf_d[(l, n)][:, c0:c0 + ncols].rearrange("(kc p) c -> p kc c", p=128)
        dma("sp", wA_sem[i], wA[i].ap[:, :, 0:ncols], src, wbfT[(l, n)], [wA[i]])
        return wA[i]

    def load_wB(l, n, c0, ncols, nkc):
        i = wBrot.next()
        src = wbf_d[(l, n)][:, c0:c0 + ncols].rearrange("(kc p) c -> p kc c", p=128)
        if ncols == 512:
            dst = wB[i].ap[:, 0:nkc, :]
        else:
            dst = wB[i].ap.rearrange("p a b -> p (a b)")[:, 0:nkc * ncols].rearrange("p (a b) -> p a b", a=nkc)
        dma("sp", wB_sem[i], dst, src, wbfT[(l, n)], [wB[i]])
        return wB[i], dst

    def stage_load(G):
        if G["meta"]:
            dma("sp", xsem, h.ap[0:NMETA, 0, :], meta_d, (), hts)
        else:
            src = x_d[G["row0"]:G["row0"] + GT, :].rearrange("(i p) d -> p i d", p=128)
            dma("sp", xsem, h.ap, src, (), hts)

    stg = [(cacc.items[0], cacc.items[1]), (raw.items[0], raw.items[1])]
    stg_sem = [(S.new_dma_sem(), S.new_dma_sem()), (S.new_dma_sem(), S.new_dma_sem())]

    def prefetch_xT(Gn):
        S.label = "load"
        for ti, nt in enumerate(Gn["nt"]):
            r0 = Gn["row0"] + ti * 128
            sl = ti % 2
            for half in range(2):
                tb_ = stg[sl][half]
                dma("sp", stg_sem[sl][half], tb_.ap[0:nt, 0:512], x_d[r0:r0 + nt, half * 512:(half + 1) * 512], (), [tb_])
                pt = tprot.next()
                for c in range(4):
                    tr(pt.ap[:, c * 128:c * 128 + nt], tb_.ap[0:nt, c * 128:(c + 1) * 128], ident.ap[0:nt, 0:nt], [tb_, ident], [pt])
                src = pt.ap.rearrange("p (c t) -> p c t", t=128)[:, :, 0:nt]
                cp("act", xT.ap[:, half * 4:half * 4 + 4, ti * 128:ti * 128 + nt], src, [pt], [xT])

    def stage_ffn(G, l, which, prefetch=None, late_load=None):
        S.label = "ffn_gu"
        NT = sum(G["nt"])
        ntl = G["nt"]
        pre = "ffn%d_" % which
        for blk in range(11):
            wg = load_wA(l, pre + "w_gate", blk * 256)
            wu = load_wA(l, pre + "w_up", blk * 256)
            for cc in range(2):
                ch = blk * 2 + cc
                pg = mmrot.next()
                pu = mmrot.next()
                for kc in range(8):
                    mm(pg.ap[:, 0:NT], wg.ap[:, kc, cc * 128:(cc + 1) * 128], xT.ap[:, kc, 0:NT], kc == 0, kc == 7, [wg, xT], [pg])
                for kc in range(8):
                    mm(pu.ap[:, 0:NT], wu.ap[:, kc, cc * 128:(cc + 1) * 128], xT.ap[:, kc, 0:NT], kc == 0, kc == 7, [wu, xT], [pu])
                sg = ftmp.next()
                act(sg.ap[:, 0:NT], pg.ap[:, 0:NT], AF.Silu, [pg], [sg])
                tt("dve", actT.ap[:, ch, 0:NT], sg.ap[:, 0:NT], pu.ap[:, 0:NT], ALU.mult, [sg, pu], [actT])
        if late_load is not None:
            S.label = "load"
            stage_load(late_load)
        if prefetch is not None:
            prefetch_xT(prefetch)
        S.label = "ffn_down"
        nti = len(ntl)
        nt0 = ntl[0]

        def tail(dch, ys):
            pt = tprot.next()
            for ti, nt in enumerate(ntl):
                tr(pt.ap[0:nt, ti * 128:(ti + 1) * 128], ys.ap[:, ti * 128:ti * 128 + nt], ident.ap, [ys, ident], [pt])
            hv = h.ap[0:nt0, 0:nti, dch * 128:(dch + 1) * 128]
            pv = pt.ap[0:nt0, 0:nti * 128].rearrange("p (i c) -> p i c", c=128)
            stt("dve", hv, pv, C_FFN, hv, ALU.mult, ALU.add, [pt] + hts[0:nti], hts[0:nti])
        pending = None
        for blk in range(4):
            i = wDrot.next()
            src = wbf_d[(l, pre + "w_down")][:, blk * 256:(blk + 1) * 256].rearrange("(fc p) c -> p fc c", p=128)
            dma("sp", wD_sem[i], wD[i].ap, src, wbfT[(l, pre + "w_down")], [wD[i]])
            for cc in range(2):
                dch = blk * 2 + cc
                py = mmrot.next()
                for fc in range(NFF):
                    mm(py.ap[:, 0:NT], wD[i].ap[:, fc, cc * 128:(cc + 1) * 128], actT.ap[:, fc, 0:NT], fc == 0, fc == NFF - 1, [wD[i], actT], [py])
                ys = ftmp.next()
                cp("act", ys.ap[:, 0:NT], py.ap[:, 0:NT], [py], [ys])
                if pending is not None:
                    tail(*pending)
                pending = (dch, ys)
        tail(*pending)

    def stage_ln(G, l, k, final):
        S.label = "ln"
        ntl = G["nt"]
        gname, bname = ["ln1_g", "ln2_g", "ln3_g"][k], ["ln1_b", "ln2_b", "ln3_b"][k]
        pg0 = l * PPW + 56 + (2 * k) * 8
        pb0 = l * PPW + 56 + (2 * k + 1) * 8
        i = lnprot.next()
        dma("sp", lnp_sem[i][0], lnp[i].ap[:, 0, :], bc_d[gname][l:l + 1, :].partition_broadcast(128), (), [lnpg[i]])
        dma("sp", lnp_sem[i][1], lnp[i].ap[:, 1, :], bc_d[bname][l:l + 1, :].partition_broadcast(128), (), [lnpb[i]])
        def front(ti, nt):
            hv = h.ap[0:nt, ti, :]
            S.op("dve", lambda e, nt=nt, ti=ti: e.bn_stats(stats.ap[0:nt, 0, :], h.ap[0:nt, ti, 0:512]), [hts[ti]], [stats])
            S.op("dve", lambda e, nt=nt, ti=ti: e.bn_stats(stats.ap[0:nt, 1, :], h.ap[0:nt, ti, 512:1024]), [hts[ti]], [stats])
            s0 = sm.next()
            S.op("dve", lambda e, nt=nt, s0=s0: e.bn_aggr(s0.ap[0:nt, 0:2], stats.ap[0:nt, 0:2, :]), [stats], [s0])
            act(s0.ap[0:nt, 2:3], s0.ap[0:nt, 1:2], AF.Ln, [s0, cst], [s0], bias=cst.ap[0:nt, 0:1], scale=1.0)
            act(s0.ap[0:nt, 3:4], s0.ap[0:nt, 2:3], AF.Exp, [s0], [s0], scale=-0.5)
            ts("dve", hv, hv, s0.ap[0:nt, 0:1], s0.ap[0:nt, 3:4], ALU.subtract, ALU.mult, [hts[ti], s0], [hts[ti]])

        def back(ti, nt):
            hv = h.ap[0:nt, ti, :]
            if not final:
                for half in range(2):
                    pt = tprot.next()
                    for c in range(4):
                        ch = half * 4 + c
                        tr(pt.ap[:, c * 128:c * 128 + nt], h.ap[0:nt, ti, ch * 128:(ch + 1) * 128], ident.ap[0:nt, 0:nt], [hts[ti], ident], [pt])
                    for c in range(4):
                        ch = half * 4 + c
                        if c != 3:
                            act(xT.ap[:, ch, ti * 128:ti * 128 + nt], pt.ap[:, c * 128:c * 128 + nt], AF.Identity, [pt, pp], [xT],
                                bias=pp.ap[:, pb0 + ch:pb0 + ch + 1], scale=pp.ap[:, pg0 + ch:pg0 + ch + 1])
                        else:
                            ts("dve", xT.ap[:, ch, ti * 128:ti * 128 + nt], pt.ap[:, c * 128:c * 128 + nt],
                               pp.ap[:, pg0 + ch:pg0 + ch + 1], pp.ap[:, pb0 + ch:pb0 + ch + 1], ALU.mult, ALU.add, [pt, pp], [xT])
            tt("pool", hv, hv, lnp[i].ap[0:nt, 0, :], ALU.mult, [hts[ti], lnpg[i]], [hts[ti]])
            tt("pool", hv, hv, lnp[i].ap[0:nt, 1, :], ALU.add, [hts[ti], lnpb[i]], [hts[ti]])
            if final and not G["meta"]:
                r0 = G["row0"] + ti * 128
                dma("pool", osem[ti], out_d[r0:r0 + nt, :], hv, [hts[ti]], [])

        for ti, nt in enumerate(ntl):
            front(ti, nt)
            if ti >= 1:
                back(ti - 1, ntl[ti - 1])
        back(len(ntl) - 1, ntl[-1])

    def stage_proj_feat(G, l):
        S.label = "proj_feat"
        NT = sum(G["nt"])
        ppb = l * PPW
        for blk in range(4):
            w = load_wA(l, "w_in", O_MQ + blk * 256)
            for cc in range(2):
                ch = blk * 2 + cc
                p = mmrot.next()
                for kc in range(8):
                    mm(p.ap[:, 0:NT], w.ap[:, kc, cc * 128:(cc + 1) * 128], xT.ap[:, kc, 0:NT], kc == 0, kc == 7, [w, xT], [p])
                r = raw.next()
                wcol = lambda j: pp.ap[:, ppb + ch * 4 + j:ppb + ch * 4 + j + 1]
                cp("dve", r.ap[:, 0:3], hist[l].ap[:, ch, :], [hist[l]], [r])
                cp("act", r.ap[:, 3:3 + NT], p.ap[:, 0:NT], [p], [r])
                ca = cacc.next()
                act(ca.ap[:, 0:NT], p.ap[:, 0:NT], AF.Identity, [p, pp], [ca], scale=wcol(3))
                for j in range(3):
                    stt("dve", ca.ap[:, 0:NT], r.ap[:, j:j + NT], wcol(j), ca.ap[:, 0:NT], ALU.mult, ALU.add, [r, pp, ca], [ca])
                cp("dve", hist[l].ap[:, ch, :], r.ap[:, NT:NT + 3], [r], [hist[l]])
                act(qkT.ap[:, ch, 0:NT], ca.ap[:, 0:NT], AF.Silu, [ca, pp], [qkT], bias=pp.ap[:, ppb + 32 + ch:ppb + 32 + ch + 1], scale=1.0)
        for blk in range(4):
            w = load_wA(l, "w_in", O_AQ + blk * 256)
            for cc in range(2):
                ch = blk * 2 + cc
                p = mmrot.next()
                for kc in range(8):
                    mm(p.ap[:, 0:NT], w.ap[:, kc, cc * 128:(cc + 1) * 128], xT.ap[:, kc, 0:NT], kc == 0, kc == 7, [w, xT], [p])
                if ch < 4:
                    cp("dve", aqT.ap[:, ch, 0:NT], p.ap[:, 0:NT], [p], [aqT])
                else:
                    cp("act", akT.ap[:, ch - 4, 0:NT], p.ap[:, 0:NT], [p], [akT])
        if G["meta"]:
            cp("pool", metaKT[l].ap, akT.ap[:, :, 0:NMETA], [akT], [metaKT[l]])
        else:
            for hh in range(4):
                dma("pool", ksem[l][hh], kt_d[l][hh, :, G["pos0"]:G["pos0"] + NT], akT.ap[:, hh, 0:NT], [akT], [ktT[l][hh]])

    def stage_mlstm(G, l):
        S.label = "mlstm"
        ntl = G["nt"]
        nti = len(ntl)
        nt0 = ntl[0]
        wmv, wmv_v = load_wB(l, "w_in", O_MV, 512, 8)
        wav, wav_v = load_wB(l, "w_in", O_AV, 512, 8)
        src = wbf_d[(l, "w_in")][:, O_MIF:O_MIF + 8].rearrange("(kc p) c -> p kc c", p=128)
        dma("sp", wmif_sem, wmif.ap, src, wbfT[(l, "w_in")], [wmif])
        bS, bV, bK, bO0, bO1, bC0, bC1, bA = banks[6], banks[0], banks[1], banks[2], banks[3], banks[4], banks[5], banks[7]
        for ti, nt in enumerate(ntl):
            c0 = ti * 128
            for kc in range(8):
                mm(bS.ap[0:nt, ti * 16:ti * 16 + 8], xT.ap[:, kc, c0:c0 + nt], wmif.ap[:, kc, :], kc == 0, kc == 7, [xT, wmif], [bS])
        bS3 = bS.ap[0:nt0, 0:nti * 16].rearrange("p (t c) -> p t c", c=16)
        bS3f = bS.ap[:, 0:nti * 16].rearrange("p (t c) -> p t c", c=16)
        g3 = gsm.ap[0:nt0, 0:nti, :]
        tt("dve", g3[:, :, 0:8], bS3[:, :, 0:8], bif[l].ap[0:nt0, :].unsqueeze(1).to_broadcast([nt0, nti, 8]), ALU.add, [bS, bif[l]], [gsm])
        act(g3[:, :, 8:12], g3[:, :, 4:8], AF.Exp, [gsm], [gsm], scale=-1.0)
        act(g3[:, :, 12:16], g3[:, :, 8:12], AF.Ln, [gsm, cst], [gsm], bias=cst.ap[0:nt0, 3:4], scale=1.0)
        S.label = "mlstm_f32"
        for ti, nt in enumerate(ntl):
            mm(bS.ap[0:nt, ti * 16 + 8:ti * 16 + 12], umaskf.ap[0:nt, 0:nt], gsm.ap[0:nt, ti, 12:16], True, True, [umaskf, gsm], [bS])
            mm(bS.ap[:, ti * 16 + 12:ti * 16 + 16], onesf.ap[0:nt, :], gsm.ap[0:nt, ti, 12:16], True, True, [onesf, gsm], [bS])
        S.label = "mlstm"
        tt("dve", g3[:, :, 16:20], g3[:, :, 0:4], bS3[:, :, 8:12], ALU.add, [gsm, bS], [gsm])
        act(g3[:, :, 20:24], g3[:, :, 16:20], AF.Exp, [gsm, cst], [gsm], bias=cst.ap[0:nt0, 2:3], scale=1.0)
        act(g3[:, :, 24:28], bS3[:, :, 8:12], AF.Exp, [bS], [gsm])
        act(gsm.ap[:, 0:nti, 28:32], bS3f[:, :, 12:16], AF.Exp, [bS], [gsm], scale=-1.0)
        for ti, nt in enumerate(ntl):
            c0 = ti * 128
            ew = gsm.ap[0:nt, ti, 20:24]
            emb = gsm.ap[0:nt, ti, 24:28]
            tt("dve", ctmp.ap, Cst[l].ap, gsm.ap[:, ti, 28:32].unsqueeze(2).to_broadcast([128, 4, 129]), ALU.mult, [Cst[l], gsm], [ctmp])
            for kc in range(8):
                mm(bA.ap[0:nt, :], xT.ap[:, kc, c0:c0 + nt], wav_v[:, kc, :], kc == 0, kc == 7, [xT, wav], [bA])
            cp("act", vaug.ap[0:nt, ti, :, 0:128], bA.ap[0:nt, :].rearrange("p (h c) -> p h c", c=128), [bA], [vaug])
            if G["meta"]:
                cp("pool", metaV[l].ap[0:nt, :, 0:128], vaug.ap[0:nt, ti, :, 0:128], [vaug], [metaV[l]])
            else:
                p0 = G["pos0"] + c0
                dma("pool", vsem[l][ti], v_d[l][:, p0:p0 + nt, :].rearrange("h t c -> t h c"), vaug.ap[0:nt, ti, :, :], [vaug], [vT[l][ti]])
            for kc in range(8):
                mm(bV.ap[0:nt, :], xT.ap[:, kc, c0:c0 + nt], wmv_v[:, kc, :], kc == 0, kc == 7, [xT, wmv], [bV])
            tt("dve", vw.ap[0:nt, :, 0:128], bV.ap[0:nt, :].rearrange("p (h c) -> p h c", c=128),
               ew.unsqueeze(2).to_broadcast([nt, 4, 128]), ALU.mult, [bV, gsm], [vw])
            cp("dve", vw.ap[0:nt, :, 128:129], ew.unsqueeze(2), [gsm], [vw])
            bKb = bK.ap.bitcast(BF16)
            for hh in range(4):
                tr(bKb[0:nt, hh * 128:(hh + 1) * 128], qkT.ap[:, 4 + hh, c0:c0 + nt], identb.ap, [qkT, identb], [bK])
            cp("act", ktok.ap[0:nt, :], bKb[0:nt, 0:512], [bK], [ktok])
            for hh in range(4):
                mm(banks[7].ap[0:nt, hh * 128:hh * 128 + nt], qkT.ap[:, 4 + hh, c0:c0 + nt], qkT.ap[:, hh, c0:c0 + nt], True, True, [qkT], [banks[7]])
            tt("dve", SM.ap[0:nt, :, 0:nt], banks[7].ap[0:nt, :].rearrange("p (h c) -> p h c", c=128)[:, :, 0:nt],
               umaskb.ap[0:nt, 0:nt].unsqueeze(1).to_broadcast([nt, 4, nt]), ALU.mult, [banks[7], umaskb], [SM])
            for hh in range(4):
                bo = bO0 if hh < 2 else bO1
                o0 = (hh % 2) * 129
                mm(bo.ap[0:nt, o0:o0 + 129], SM.ap[0:nt, hh, 0:nt], vw.ap[0:nt, hh, :], True, False, [SM, vw], [bo])
                mm(bo.ap[0:nt, o0:o0 + 129], qkT.ap[:, hh, c0:c0 + nt], Cbf[l].ap[:, hh, :], False, True, [qkT, Cbf[l]], [bo])
            for hh in range(4):
                bc = bC0 if hh < 2 else bC1
                o0 = (hh % 2) * 129
                mm(bc.ap[:, o0:o0 + 129], ktok.ap[0:nt, hh * 128:(hh + 1) * 128], vw.ap[0:nt, hh, :], True, True, [ktok, vw], [bc])
            for hh in range(4):
                bc = bC0 if hh < 2 else bC1
                o0 = (hh % 2) * 129
                stt("dve", Cst[l].ap[:, hh, :], bc.ap[:, o0:o0 + 129], gsm.ap[:, ti, 28 + hh:29 + hh], ctmp.ap[:, hh, :], ALU.mult, ALU.add, [bc, gsm, ctmp], [Cst[l]])
            cp("act", Cbf[l].ap, Cst[l].ap, [Cst[l]], [Cbf[l]])
            s1 = sm.next()
            for half, bo in enumerate((bO0, bO1)):
                cp("dve", s1.ap[0:nt, half * 2:half * 2 + 2], bo.ap[0:nt, 0:258].rearrange("p (h c) -> p h c", c=129)[:, :, 128], [bo], [s1])
            stt("dve", s1.ap[0:nt, 4:8], s1.ap[0:nt, 0:4], -1.0, s1.ap[0:nt, 0:4], ALU.mult, ALU.max, [s1], [s1])
            tt("dve", s1.ap[0:nt, 8:12], s1.ap[0:nt, 4:8], emb, ALU.max, [s1, gsm], [s1])
            tt("dve", s1.ap[0:nt, 12:16], s1.ap[0:nt, 8:12], s1.ap[0:nt, 8:12], ALU.mult, [s1], [s1])
            for hh in range(4):
                bo = bO0 if hh < 2 else bO1
                o0 = (hh % 2) * 129
                S.op("dve", lambda e, nt=nt, hh=hh, bo=bo, o0=o0: e.bn_stats(stats.ap[0:nt, hh, :], bo.ap[0:nt, o0:o0 + 128]), [bo], [stats])
            for hh in range(4):
                S.op("dve", lambda e, nt=nt, hh=hh, s1=s1: e.bn_aggr(s1.ap[0:nt, 32 + 2 * hh:34 + 2 * hh], stats.ap[0:nt, hh:hh + 1, :]), [stats], [s1])
            varv = s1.ap[0:nt, 32:40].rearrange("p (h c) -> p h c", c=2)[:, :, 1]
            stt("dve", s1.ap[0:nt, 16:20], s1.ap[0:nt, 12:16], LN_EPS, varv, ALU.mult, ALU.add, [s1], [s1])
            act(s1.ap[0:nt, 20:24], s1.ap[0:nt, 16:20], AF.Ln, [s1], [s1])
            act(s1.ap[0:nt, 24:28], s1.ap[0:nt, 20:24], AF.Exp, [s1], [s1], scale=-0.5)
            for hh in range(4):
                bo = bO0 if hh < 2 else bO1
                o0 = (hh % 2) * 129
                ts("dve", hn.ap[0:nt, ti, hh * 128:(hh + 1) * 128], bo.ap[0:nt, o0:o0 + 128],
                   s1.ap[0:nt, 32 + 2 * hh:33 + 2 * hh], s1.ap[0:nt, 24 + hh:25 + hh], ALU.subtract, ALU.mult, [bo, s1], [hn])

    def stage_out_m(G, l):
        S.label = "out_a"
        ntl = G["nt"]
        wmo, wmo_v = load_wB(l, "w_in", O_MO, 512, 8)
        hb2 = [hbf, hbf1]

        def back(ti, nt, hb):
            c0 = ti * 128
            pt = tprot.next()
            ptb = pt.ap.bitcast(BF16)
            for fc in range(4):
                tr(ptb[:, fc * 128:fc * 128 + nt], hb.ap[0:nt, fc * 128:(fc + 1) * 128], identb.ap[0:nt, 0:nt], [hb, identb], [pt])
            cp("act", hmT.ap[:, :, c0:c0 + nt], ptb[:, 0:512].rearrange("p (c t) -> p c t", t=128)[:, :, 0:nt], [pt], [hmT])
        pend = None
        for ti, nt in enumerate(ntl):
            c0 = ti * 128
            hb = hb2[ti % 2]
            pm = mmrot.next()
            for kc in range(8):
                mm(pm.ap[0:nt, :], xT.ap[:, kc, c0:c0 + nt], wmo_v[:, kc, :], kc == 0, kc == 7, [xT, wmo], [pm])
            act(sgm.ap[0:nt, :], pm.ap[0:nt, :], AF.Sigmoid, [pm], [sgm])
            tt("dve", hmtmp.ap[0:nt, :], hn.ap[0:nt, ti, :], sgm.ap[0:nt, :], ALU.mult, [hn, sgm], [hmtmp])
            tt("dve", hb.ap[0:nt, :], hmtmp.ap[0:nt, :], mng[l].ap[0:nt, :], ALU.mult, [hmtmp, mng[l]], [hb])
            if pend is not None:
                back(*pend)
            pend = (ti, nt, hb)
        back(*pend)

    attn_tail_cb = [None]

    def stage_attn(G, l):
        S.label = "attn"
        ntl = G["nt"]
        NT = sum(ntl)
        nti = len(ntl)
        nt0_ = ntl[0]
        pos0 = G["pos0"]
        g = G["g"]
        sbanks = Rot([(banks[0], banks[1]), (banks[2], banks[3])])
        def accv(qt, c):
            a_ = qt * 2 + c
            return banks[4 + a_ // 3], (a_ % 3) * 129
        tpb = banks[7]
        mset("dve", ssq.ap, 0.0, [ssq])
        deferred = [None]

        def head_tail1(hh):
            S.label = "attn_fin"
            for qt in range(nti):
                ntq = ntl[qt]
                s0 = sm.next()
                act(s0.ap[0:ntq, 0:1], ssq.ap[0:ntq, qt, hh:hh + 1], AF.Ln, [ssq, cst], [s0], bias=cst.ap[0:ntq, 1:2], scale=1.0 / 128.0)
                act(s0.ap[0:ntq, 1:2], s0.ap[0:ntq, 0:1], AF.Exp, [s0], [s0], scale=-0.5)
                stt("dve", habf.ap[0:ntq, qt, :], ao.ap[0:ntq, qt, hh * 128:(hh + 1) * 128], s0.ap[0:ntq, 1:2],
                    ang[l].ap[0:ntq, hh * 128:(hh + 1) * 128], ALU.mult, ALU.mult, [ao, s0, ang[l]], [habf])

        def head_tail2(hh):
            S.label = "attn_tail"
            tb = tpb.ap.bitcast(BF16)
            for qt in range(nti):
                ntq = ntl[qt]
                tr(tb[:, qt * 128:qt * 128 + ntq], habf.ap[0:ntq, qt, :], identb.ap[0:ntq, 0:ntq], [habf, identb], [tpb])
            cp("act", haT.ap[:, hh, 0:NT], tb[:, 0:NT], [tpb], [haT])

        for hh in range(4):
            slope = SLOPES[hh]
            bank_started = {}
            started = [[False, False] for _ in range(nti)]
            units = []
            if not G["meta"]:
                if slope * (pos0 - NMETA + 1) <= WIN_CUT:
                    colB = hh * (NA_M + 17) + NA_M + g
                    units.append(("sb", metaKT[l].ap[:, hh, :], metaV[l].ap[0:NMETA, hh, :], NMETA, colB, 0, [metaKT[l], metaV[l]], None))
                nprev = 4 * g
                u_lo = 0
                while u_lo < nprev and slope * (pos0 - (NMETA + 128 * (u_lo + 1)) + 1) > WIN_CUT:
                    u_lo += 1
                u = u_lo
                while u < nprev:
                    n_u = min(16, nprev - u)
                    for j in range(n_u):
                        m = nprev - (u + j)
                        colA = hh * (NA_M + 17) + (m + 3)
                        units.append(("dram", u, n_u, j, colA))
                    u += n_u
            for kt in range(nti):
                nk = ntl[kt]
                if G["meta"]:
                    colA = hh * (NA_M + 17) + NA_M + 16
                else:
                    colA = hh * (NA_M + 17) + (-kt + 3)
                units.append(("sb", akT.ap[:, hh, kt * 128:kt * 128 + nk], vaug.ap[0:nk, kt, hh, :], nk, colA, kt * 128, [akT, vaug], kt))

            cur_chunk = [None]

            def resolve(un):
                if un[0] == "sb":
                    return un[1:]
                _, u0, n_u, j, colA = un
                if cur_chunk[0] is None or cur_chunk[0][0] != u0:
                    bi = KVrot.next()
                    p_lo = NMETA + 128 * u0
                    dma("sp", KV_sem[bi][0], KTc[bi].ap[:, 0:128 * n_u], kt_d[l][hh, :, p_lo:p_lo + 128 * n_u], [ktT[l][hh]], [KTc[bi]])
                    dma("sp", KV_sem[bi][1], Vc[bi].ap[:, 0:n_u, :], v_d[l][hh, p_lo:p_lo + 128 * n_u, :].rearrange("(u p) c -> p u c", p=128), vT[l], [Vc[bi]])
                    cur_chunk[0] = (u0, bi)
                bi = cur_chunk[0][1]
                return (KTc[bi].ap[:, j * 128:(j + 1) * 128], Vc[bi].ap[:, j, :], 128, colA, 0, [KTc[bi], Vc[bi]], None)

            def phaseA(un):
                S.label = "attn_S"
                kT_ap, v_ap, nk, colA, q0, kdeps, diag_kt = resolve(un)
                sp = sbanks.next()
                pts = []
                for c in range(2):
                    bk = sp[c]
                    rows = slice(64 * c, 64 * c + 64)
                    if diag_kt is None:
                        mm(bk.ap[0:nk, q0:NT], kT_ap[rows, :], aqT.ap[rows, hh, q0:NT], True, True, kdeps + [aqT], [bk])
                    else:
                        ntq = ntl[diag_kt]
                        mm(bk.ap[0:nk, q0:q0 + ntq], kT_ap[rows, :], aqT.ap[rows, hh, q0:q0 + ntq], True, False, kdeps + [aqT], [bk])
                        mm(bk.ap[0:nk, q0:q0 + ntq], identb.ap[0:nk, 0:nk], mnegb.ap[0:nk, 0:ntq], False, True, [identb, mnegb], [bk])
                        if q0 + 128 < NT:
                            mm(bk.ap[0:nk, q0 + 128:NT], kT_ap[rows, :], aqT.ap[rows, hh, q0 + 128:NT], True, True, kdeps + [aqT], [bk])
                    pt = PT.next()
                    act(pt.ap[0:nk, q0:NT], bk.ap[0:nk, q0:NT], AF.Exp, [bk, alibi], [pt], bias=alibi.ap[0:nk, colA:colA + 1], scale=0.125)
                    pts.append(pt)
                return (pts, v_ap, nk, q0, kdeps, diag_kt)

            def phaseB(st):
                S.label = "attn_PV"
                pts, v_ap, nk, q0, kdeps, diag_kt = st
                for c in range(2):
                    pt = pts[c]
                    for qt in range(q0 // 128, nti):
                        ntq = ntl[qt]
                        last = (diag_kt is not None and qt == diag_kt)
                        ab, col = accv(qt, c)
                        first = ab.name not in bank_started
                        bank_started[ab.name] = True
                        S.op("pe", lambda e, o_=ab.ap[0:ntq, col:col + 129], l_=pt.ap[0:nk, qt * 128:qt * 128 + ntq], r_=v_ap, f_=first, s_=last:
                             e.matmul(o_, l_, r_, start=f_, stop=s_, skip_group_check=True), kdeps + [pt], [ab])

            prev = None
            defer_at = min(3, len(units) - 1)
            for ui, un in enumerate(units):
                st = phaseA(un)
                if ui == defer_at and deferred[0] is not None:
                    head_tail2(deferred[0])
                    deferred[0] = None
                if prev is not None:
                    phaseB(prev)
                prev = st
            phaseB(prev)
            S.label = "attn_fin"
            nacc = 2 * nti
            cpy = []
            for bi_ in range((nacc + 2) // 3):
                nsl = min(3, nacc - 3 * bi_)
                fb = ftmp.next()
                cp("act" if bi_ % 2 == 0 else "dve", fb.ap[0:nt0_, 0:129 * nsl], banks[4 + bi_].ap[0:nt0_, 0:129 * nsl], [banks[4 + bi_]], [fb])
                cpy.append(fb)

            def accs(qt, c):
                a_ = qt * 2 + c
                return cpy[a_ // 3], (a_ % 3) * 129
            for qt in range(nti):
                ntq = ntl[qt]
                ab0, c0_ = accs(qt, 0)
                ab1, c1_ = accs(qt, 1)
                S.op("dve", lambda e, ntq=ntq, ab0=ab0, c0_=c0_: e.reciprocal(rr.ap[0:ntq, 0:1], ab0.ap[0:ntq, c0_ + 128:c0_ + 129]), [ab0], [rr])
                S.op("dve", lambda e, ntq=ntq, ab1=ab1, c1_=c1_: e.reciprocal(rr.ap[0:ntq, 1:2], ab1.ap[0:ntq, c1_ + 128:c1_ + 129]), [ab1], [rr])
                tt("dve", rr.ap[0:ntq, 2:3], rr.ap[0:ntq, 1:2], lamt[l].ap[0:ntq, 0:1], ALU.mult, [rr, lamt[l]], [rr])
                ov = ao.ap[0:ntq, qt, hh * 128:(hh + 1) * 128]
                ts("dve", ov, ab0.ap[0:ntq, c0_:c0_ + 128], rr.ap[0:ntq, 0:1], None, ALU.mult, None, [ab0, rr], [ao])
                stt("dve", ov, ab1.ap[0:ntq, c1_:c1_ + 128], rr.ap[0:ntq, 2:3], ov, ALU.mult, ALU.add, [ab1, rr, ao], [ao])
                act(hmtmp.ap[0:ntq, 0:128], ov, AF.Square, [ao], [hmtmp, ssq], accum=ssq.ap[0:ntq, qt, hh:hh + 1])
            head_tail1(hh)
            deferred[0] = hh
        attn_tail_cb[0] = (lambda hh_=deferred[0]: head_tail2(hh_))

    def stage_out(G, l):
        ntl = G["nt"]
        NT = sum(ntl)
        ppb = l * PPW
        S.label = "out_b"
        wbm, wbm_v = load_wB(l, "w_bm", 0, 1024, 4)
        wba, wba_v = load_wB(l, "w_ba", 0, 1024, 4)
        for j in range(4):
            wgm = load_wA(l, "w_in", O_G + j * 256)
            wga = load_wA(l, "w_in", O_G + 1024 + j * 256)
            for cc in range(2):
                c = 2 * j + cc
                pgm, pga, pym, pya = mmrot.next(), mmrot.next(), mmrot.next(), mmrot.next()
                for kc in range(8):
                    mm(pgm.ap[:, 0:NT], wgm.ap[:, kc, cc * 128:(cc + 1) * 128], xT.ap[:, kc, 0:NT], kc == 0, kc == 7, [wgm, xT], [pgm])
                for kc in range(8):
                    mm(pga.ap[:, 0:NT], wga.ap[:, kc, cc * 128:(cc + 1) * 128], xT.ap[:, kc, 0:NT], kc == 0, kc == 7, [wga, xT], [pga])
                for fc in range(4):
                    mm(pym.ap[:, 0:NT], wbm_v[:, fc, c * 128:(c + 1) * 128], hmT.ap[:, fc, 0:NT], fc == 0, fc == 3, [wbm, hmT], [pym])
                if attn_tail_cb[0] is not None:
                    attn_tail_cb[0]()
                    attn_tail_cb[0] = None
                    S.label = "out_b"
                for fc in range(4):
                    mm(pya.ap[:, 0:NT], wba_v[:, fc, c * 128:(c + 1) * 128], haT.ap[:, fc, 0:NT], fc == 0, fc == 3, [wba, haT], [pya])
                s_m = ftmp.next()
                s_a = ftmp.next()
                act(s_m.ap[:, 0:NT], pgm.ap[:, 0:NT], AF.Sigmoid, [pgm, pp], [s_m], bias=pp.ap[:, ppb + 40 + c:ppb + 41 + c], scale=1.0)
                act(s_a.ap[:, 0:NT], pga.ap[:, 0:NT], AF.Sigmoid, [pga, pp], [s_a], bias=pp.ap[:, ppb + 48 + c:ppb + 49 + c], scale=1.0)
                tt("dve", s_m.ap[:, 0:NT], s_m.ap[:, 0:NT], pym.ap[:, 0:NT], ALU.mult, [s_m, pym], [s_m])
                tt("dve", s_a.ap[:, 0:NT], s_a.ap[:, 0:NT], pya.ap[:, 0:NT], ALU.mult, [s_a, pya], [s_a])
                tt("dve", zT.ap[:, c, 0:NT], s_m.ap[:, 0:NT], s_a.ap[:, 0:NT], ALU.add, [s_m, s_a], [zT])
        S.label = "out_c"
        wo = []
        for half in range(2):
            wo.append(load_wB(l, "w_out", half * 512, 512, 8))
        for ti, nt in enumerate(ntl):
            c0 = ti * 128
            for half in range(2):
                pm = mmrot.next()
                for kc in range(8):
                    mm(pm.ap[0:nt, :], zT.ap[:, kc, c0:c0 + nt], wo[half][1][:, kc, :], kc == 0, kc == 7, [zT, wo[half][0]], [pm])
                hv = h.ap[0:nt, ti, half * 512:(half + 1) * 512]
                stt("dve", hv, pm.ap[0:nt, :], C_MIX, hv, ALU.mult, ALU.add, [pm, hts[ti]], [hts[ti]])

    def initial_xT(G):
        ntl = G["nt"]
        for ti, nt in enumerate(ntl):
            for half in range(2):
                pt = tprot.next()
                for c in range(4):
                    ch = half * 4 + c
                    tr(pt.ap[:, c * 128:c * 128 + nt], h.ap[0:nt, ti, ch * 128:(ch + 1) * 128], ident.ap[0:nt, 0:nt], [hts[ti], ident], [pt])
                src = pt.ap.rearrange("p (c t) -> p c t", t=128)[:, :, 0:nt]
                cp("act", xT.ap[:, half * 4:half * 4 + 4, ti * 128:ti * 128 + nt], src, [pt], [xT])

    for gi, G in enumerate(groups):
        S.gp = "g%d" % G["g"]
        S.label = "load"
        if gi == 0:
            stage_load(G)
            initial_xT(G)
        Gn = groups[gi + 1] if gi + 1 < len(groups) else None
        for l in range(DEPTH):
            stage_ffn(G, l, 1, late_load=(G if (l == 0 and gi > 0) else None))
            stage_ln(G, l, 0, False)
            stage_proj_feat(G, l)
            stage_mlstm(G, l)
            stage_out_m(G, l)
            stage_attn(G, l)
            stage_out(G, l)
            stage_ln(G, l, 1, False)
            stage_ffn(G, l, 2, prefetch=(Gn if l == DEPTH - 1 else None))
            stage_ln(G, l, 2, final=(l == DEPTH - 1))
    S.wait_all("pool", osem)
    S.wait_all("sp", osem)
    print("instr counts:", {k: v for k, v in S.cnt.items() if k in Sched.ENGS}, "sems:", len(S.sems))
    S.emit()
    nc._labels = S.labels
    return nc


def make_in_maps(inputs, NG, ncores):
    consts = host_consts()
    x = np.asarray(inputs["x"], dtype=np.float32)
    pp = np.zeros((128, DEPTH, PPW), dtype=np.float32)
    cw = np.asarray(inputs["conv_w"], dtype=np.float32)
    cb = np.asarray(inputs["conv_b"], dtype=np.float32)
    bg = np.asarray(inputs["b_gate"], dtype=np.float32)
    pp[:, :, 0:32] = cw.reshape(DEPTH, 4, 8, 128).transpose(3, 0, 2, 1).reshape(128, DEPTH, 32)
    pp[:, :, 32:40] = cb.reshape(DEPTH, 8, 128).transpose(2, 0, 1)
    pp[:, :, 40:56] = bg.reshape(DEPTH, 16, 128).transpose(2, 0, 1)
    for k_, n_ in enumerate(["ln1_g", "ln1_b", "ln2_g", "ln2_b", "ln3_g", "ln3_b"]):
        v_ = np.asarray(inputs[n_], dtype=np.float32)
        pp[:, :, 56 + k_ * 8:56 + (k_ + 1) * 8] = v_.reshape(DEPTH, 8, 128).transpose(2, 0, 1)
    pp = np.ascontiguousarray(pp.reshape(128, DEPTH * PPW))
    shared = {n: np.ascontiguousarray(np.asarray(inputs[n], dtype=np.float32)) for n in WNAMES}
    for n in BC_NAMES:
        shared[n] = np.ascontiguousarray(np.asarray(inputs[n], dtype=np.float32))
    shared["meta"] = np.ascontiguousarray(np.asarray(inputs["meta"], dtype=np.float32))
    shared["pp"] = pp
    shared.update(consts)
    maps = []
    for c in range(ncores):
        m = dict(shared)
        m["x"] = np.ascontiguousarray(x[c, :GT * NG, :])
        maps.append(m)
    return maps


_NC_CACHE = {}


def kernel(**inputs):
    NG = 16
    ncores = 8
    if NG not in _NC_CACHE:
        _NC_CACHE[NG] = build_nc(NG)
    nc = _NC_CACHE[NG]
    maps = make_in_maps(inputs, NG, ncores)
    res = run_bass_kernel_spmd(nc, maps, core_ids=list(range(ncores)))
    out = np.stack([np.asarray(r["out"], dtype=np.float32) for r in res.results], axis=0)
    return out
```

```python
import math
import numpy as np
import concourse.bass as bass
import concourse.mybir as mybir
from concourse.bass_utils import run_bass_kernel_spmd

AF = mybir.ActivationFunctionType
ALU = mybir.AluOpType
F32 = mybir.dt.float32
BF16 = mybir.dt.bfloat16

D = 1024
FF = 2816
NFF = 22
DP = 5640
NMETA = 16
DEPTH = 2
GT = 512
ALPHA = (2 * DEPTH) ** 0.25
LN_EPS = 1e-5
EPS_RES = LN_EPS / (ALPHA * ALPHA)
C_FFN = 0.5 / ALPHA
C_MIX = 1.0 / ALPHA
SLOPES = [2.0 ** (-8.0 * (h + 1) / 4) for h in range(4)]
WIN_CUT = 60.0
NEGM = -30000.0
LN_KSCALE = math.log(128 ** -0.5)
O_MQ, O_MK, O_MV, O_MO, O_MIF, O_AQ, O_AK, O_AV, O_G = 0, 512, 1024, 1536, 2048, 2056, 2568, 3080, 3592


class T:
    __slots__ = ("ap", "lastw", "readers", "name", "al")

    def __init__(self, ap, name=""):
        self.ap = ap
        self.lastw = None
        self.readers = []
        self.name = name
        self.al = ()

    def __getitem__(self, k):
        return self.ap[k]


class Sched:
    ENGS = ("pe", "act", "dve", "pool", "sp")

    def __init__(self, nc):
        self.nc = nc
        self.ops = {e: [] for e in self.ENGS}
        self.sems = {}
        self.cnt = {}
        for e in self.ENGS:
            self.sems[e] = nc.alloc_semaphore("c_" + e)
            self.cnt[e] = 0
        self.known = {e: {} for e in self.ENGS}
        self.nslot = 0
        self.label = ""
        self.gp = "init"
        self.labels = {e: [] for e in self.ENGS}

    def new_dma_sem(self):
        k = "d%d" % self.nslot
        self.nslot += 1
        self.sems[k] = self.nc.alloc_semaphore(k)
        self.cnt[k] = 0
        return k

    def _deps(self, eng, reads, writes):
        need = {}

        def add(tok):
            k, v = tok
            if need.get(k, 0) < v:
                need[k] = v
        for t in reads:
            if t.lastw is not None:
                add(t.lastw)
            for a in t.al:
                if a.lastw is not None:
                    add(a.lastw)
        for t in writes:
            if t.lastw is not None:
                add(t.lastw)
            for r in t.readers:
                add(r)
            for a in t.al:
                if a.lastw is not None:
                    add(a.lastw)
                for r in a.readers:
                    add(r)
        waits = []
        kn = self.known[eng]
        for k, v in need.items():
            if k == eng and eng == "pe":
                continue
            if kn.get(k, 0) < v:
                kn[k] = v
                waits.append((k, v))
        return waits

    def _post(self, tok, reads, writes):
        for t in reads:
            if len(t.readers) > 24:
                mx = {}
                for k, v in t.readers:
                    if mx.get(k, 0) < v:
                        mx[k] = v
                t.readers = list(mx.items())
            t.readers.append(tok)
        for t in writes:
            t.lastw = tok
            t.readers = []

    def op(self, eng, fn, reads=(), writes=()):
        waits = self._deps(eng, reads, writes)
        self.cnt[eng] += 1
        tok = (eng, self.cnt[eng])
        self.ops[eng].append((waits, fn, eng, 1))
        self.labels[eng].append(self.gp + "/" + self.label)
        self._post(tok, reads, writes)
        return tok

    def dma(self, q, semk, fn, reads=(), writes=()):
        waits = self._deps(q, reads, writes)
        kn = self.known[q]
        if self.cnt[semk] > 0 and kn.get(semk, 0) < self.cnt[semk]:
            kn[semk] = self.cnt[semk]
            waits.append((semk, self.cnt[semk]))
        self.cnt[semk] += 16
        tok = (semk, self.cnt[semk])
        self.ops[q].append((waits, fn, semk, 16))
        self.labels[q].append(self.gp + "/" + self.label)
        self._post(tok, reads, writes)
        return tok

    def wait_all(self, eng, keys):
        kn = self.known[eng]
        waits = []
        for k in keys:
            v = self.cnt[k]
            if v > 0 and kn.get(k, 0) < v:
                kn[k] = v
                waits.append((k, v))
        if waits:
            self.ops[eng].append((waits, None, None, 0))

    def emit(self):
        nc = self.nc
        sems = self.sems
        with nc.Block() as block:
            def runner(name):
                def body(e):
                    for waits, fn, semk, inc in self.ops[name]:
                        for (k, v) in waits:
                            e.wait_ge(sems[k], v)
                        if fn is not None:
                            fn(e).then_inc(sems[semk], inc)
                return body
            block.tensor(runner("pe"))
            block.scalar(runner("act"))
            block.vector(runner("dve"))
            block.gpsimd(runner("pool"))
            block.sync(runner("sp"))


class Rot:
    def __init__(self, items):
        self.items = items
        self.i = 0

    def next(self):
        x = self.items[self.i % len(self.items)]
        self.i += 1
        return x


WNAMES = ["ffn1_w_gate", "ffn1_w_up", "ffn1_w_down", "w_in", "w_bm", "w_ba", "w_out",
          "ffn2_w_gate", "ffn2_w_up", "ffn2_w_down"]
WSHAPES = {"ffn1_w_gate": (D, FF), "ffn1_w_up": (D, FF), "ffn1_w_down": (FF, D), "w_in": (D, DP),
           "w_bm": (512, D), "w_ba": (512, D), "w_out": (D, D),
           "ffn2_w_gate": (D, FF), "ffn2_w_up": (D, FF), "ffn2_w_down": (FF, D)}
BC_NAMES = {"ln1_g": D, "ln1_b": D, "ln2_g": D, "ln2_b": D, "ln3_g": D, "ln3_b": D,
            "m_norm_g": 512, "a_norm_g": 512, "b_if": 8,
            "lam_q1": 64, "lam_k1": 64, "lam_q2": 64, "lam_k2": 64}
NA_M = 68
PPW = 104
DEBUG = False


def host_consts():
    c = {}
    c["ident"] = np.eye(128, dtype=np.float32)
    s = np.arange(128)[:, None]
    t = np.arange(128)[None, :]
    c["umask"] = (s <= t).astype(np.float32)
    c["mneg"] = np.where(s > t, NEGM, 0.0).astype(np.float32)
    p = np.arange(128, dtype=np.float64)[:, None, None]
    sl = np.array(SLOPES, dtype=np.float64)[None, :, None]
    m = np.arange(-3, NA_M - 3, dtype=np.float64)[None, None, :]
    tabA = sl * (p - 128.0 * m - 256.0)
    g = np.arange(16, dtype=np.float64)[None, None, :]
    tabB = sl * (p - 272.0 - 512.0 * g)
    tabC = sl * p * np.ones((1, 1, 1))
    c["alibi"] = np.concatenate([tabA, tabB, tabC], axis=2).reshape(128, -1).astype(np.float32)
    return c


def build_nc(NG):
    L = NMETA + GT * NG
    nc = bass.Bass("TRN2", target_bir_lowering=False)
    S = Sched(nc)

    def din(name, shape, dt=F32):
        return nc.dram_tensor(name, list(shape), dt, kind="ExternalInput").ap()

    x_d = din("x", (GT * NG, D))
    meta_d = din("meta", (NMETA, D))
    w_d = {n: din(n, (DEPTH,) + WSHAPES[n]) for n in WNAMES}
    bc_d = {n: din(n, (DEPTH, BC_NAMES[n])) for n in BC_NAMES}
    pp_d = din("pp", (128, DEPTH * PPW))
    ident_d = din("ident", (128, 128))
    umask_d = din("umask", (128, 128))
    mneg_d = din("mneg", (128, 128))
    NAL = 4 * (NA_M + 16 + 1)
    alibi_d = din("alibi", (128, NAL))
    out_d = nc.dram_tensor("out", [GT * NG, D], F32, kind="ExternalOutput").ap()
    if DEBUG:
        dbg_hn = nc.dram_tensor("dbg_hn", [NMETA + GT, 512], F32, kind="ExternalOutput").ap()
        dbg_ao = nc.dram_tensor("dbg_ao", [NMETA + GT, 512], F32, kind="ExternalOutput").ap()
        dbg_h1 = nc.dram_tensor("dbg_h1", [NMETA + GT, D], F32, kind="ExternalOutput").ap()
    wbf_d = {}
    wbfT = {}
    for l in range(DEPTH):
        for n in WNAMES:
            wbf_d[(l, n)] = nc.dram_tensor("wbf_%d_%s" % (l, n), list(WSHAPES[n]), BF16, kind="Internal").ap()
            wbfT[(l, n)] = [T(wbf_d[(l, n)], "wbf%d" % i_) for i_ in range(4)]
    kt_d = [nc.dram_tensor("ktc%d" % l, [4, 128, L], BF16, kind="Internal").ap() for l in range(DEPTH)]
    v_d = [nc.dram_tensor("vc%d" % l, [4, L, 129], BF16, kind="Internal").ap() for l in range(DEPTH)]
    ktT = [[T(kt_d[l], "ktd") for _ in range(4)] for l in range(DEPTH)]
    vT = [[T(v_d[l], "vd") for _ in range(4)] for l in range(DEPTH)]

    def sb(name, shape, dt):
        return T(nc.alloc_sbuf_tensor("sb_" + name, list(shape), dt).ap(), name)

    def mm(out, lhsT, rhs, start, stop, reads, writes):
        S.op("pe", lambda e: e.matmul(out, lhsT, rhs, start=start, stop=stop), reads, writes)

    def tr(out, in_, idn, reads, writes):
        S.op("pe", lambda e: e.transpose(out, in_, idn), reads, writes)

    def act(out, in_, func, reads, writes, bias=None, scale=None, accum=None):
        kw = {}
        if bias is not None:
            kw["bias"] = bias
        if scale is not None:
            kw["scale"] = scale
        if accum is not None:
            kw["accum_out"] = accum
        S.op("act", lambda e: e.activation(out, in_, func, **kw), reads, writes)

    def tt(eng, out, in0, in1, op, reads, writes):
        S.op(eng, lambda e: e.tensor_tensor(out, in0, in1, op), reads, writes)

    def ts(eng, out, in0, s1, s2, op0, op1, reads, writes):
        if s2 is None:
            S.op(eng, lambda e: e.tensor_scalar(out, in0, s1, None, op0), reads, writes)
        else:
            S.op(eng, lambda e: e.tensor_scalar(out, in0, s1, s2, op0, op1), reads, writes)

    def stt(eng, out, in0, sc, in1, op0, op1, reads, writes):
        S.op(eng, lambda e: e.scalar_tensor_tensor(out, in0, sc, in1, op0, op1), reads, writes)

    def cp(eng, out, in_, reads, writes):
        if eng == "act":
            S.op("act", lambda e: e.activation(out, in_, AF.Copy), reads, writes)
        else:
            S.op(eng, lambda e: e.tensor_copy(out, in_), reads, writes)

    def mset(eng, out, val, writes):
        S.op(eng, lambda e: e.memset(out, val), (), writes)

    def dma(q, semk, out, in_, reads, writes):
        S.dma(q, semk, lambda e: e.dma_start(out=out, in_=in_), reads, writes)

    h = sb("h", [128, 4, D], F32)
    hts = [T(h.ap[:, i_, :], "h%d" % i_) for i_ in range(4)]
    xT = sb("xT", [128, 8, GT], BF16)
    region = nc.alloc_sbuf_tensor("region", [128, 24576], BF16).ap()

    def rview(lo, n, a, name, dt=BF16):
        v = region[:, lo:lo + n]
        if dt == F32:
            v = v.bitcast(F32)
        return T(v.rearrange("p (a b) -> p a b", a=a), name)
    actT = rview(0, 11264, NFF, "actT")
    wD = [rview(11264, 5632, NFF, "wD0"), rview(16896, 5632, NFF, "wD1")]
    qkT = rview(0, 4096, 8, "qkT")
    aqT = rview(4096, 2048, 4, "aqT")
    akT = rview(6144, 2048, 4, "akT")
    hmT = rview(8192, 2048, 4, "hmT")
    haT = rview(10240, 2048, 4, "haT")
    zT = rview(12288, 4096, 8, "zT")
    hn = rview(16384, 4096, 4, "hn", F32)
    ao = rview(20480, 4096, 4, "ao", F32)
    ffn_only = [actT] + wD
    wD = wD + [sb("wD2", [128, NFF, 256], BF16)]
    mix_only = [qkT, aqT, akT, hmT, haT, zT, hn, ao]
    for t_ in ffn_only:
        t_.al = tuple(mix_only)
    for t_ in mix_only:
        t_.al = tuple(ffn_only)
    ftmp = Rot([sb("ftmp%d" % i, [128, GT], F32) for i in range(4)])
    wA = [sb("wA%d" % i, [128, 8, 256], BF16) for i in range(4)]
    wA_sem = [S.new_dma_sem() for _ in wA]
    wArot = Rot(list(range(len(wA))))
    wD_sem = [S.new_dma_sem() for _ in wD]
    wDrot = Rot(list(range(len(wD))))
    wB = [sb("wB%d" % i, [128, 8, 512], BF16) for i in range(3)]
    wB_sem = [S.new_dma_sem() for _ in wB]
    wBrot = Rot(list(range(len(wB))))
    wmif = sb("wmif", [128, 8, 8], BF16)
    wmif_sem = S.new_dma_sem()
    lnp = [sb("lnp%d" % i, [128, 2, D], F32) for i in range(1)]
    lnpg = [T(t_.ap[:, 0, :], "lnpg") for t_ in lnp]
    lnpb = [T(t_.ap[:, 1, :], "lnpb") for t_ in lnp]
    lnp_sem = [(S.new_dma_sem(), S.new_dma_sem()) for _ in lnp]
    lnprot = Rot(list(range(len(lnp))))
    raw = Rot([sb("raw%d" % i, [128, 3 + GT], F32) for i in range(2)])
    cacc = Rot([sb("cacc%d" % i, [128, GT], F32) for i in range(2)])
    vaug = sb("vaug", [128, 4, 4, 129], BF16)
    KTc = [sb("KTc%d" % i, [128, 2048], BF16) for i in range(2)]
    Vc = [sb("Vc%d" % i, [128, 16, 129], BF16) for i in range(2)]
    KV_sem = [(S.new_dma_sem(), S.new_dma_sem()) for _ in KTc]
    KVrot = Rot(list(range(len(KTc))))
    PT = Rot([sb("PT%d" % i, [128, GT], BF16) for i in range(4)])
    vw = sb("vw", [128, 4, 129], BF16)
    ktok = sb("ktok", [128, 512], BF16)
    SM = sb("SM", [128, 4, 128], BF16)
    Cst = [sb("Cst%d" % l, [128, 4, 129], F32) for l in range(DEPTH)]
    Cbf = [sb("Cbf%d" % l, [128, 4, 129], BF16) for l in range(DEPTH)]
    ctmp = sb("ctmp", [128, 4, 129], F32)
    hist = [sb("hist%d" % l, [128, 8, 3], F32) for l in range(DEPTH)]
    metaKT = [sb("metaKT%d" % l, [128, 4, 16], BF16) for l in range(DEPTH)]
    metaV = [sb("metaV%d" % l, [128, 4, 129], BF16) for l in range(DEPTH)]
    sm = Rot([sb("sm%d" % i, [128, 64], F32) for i in range(3)])
    stats = sb("stats", [128, 8, 6], F32)
    gsm = sb("gsm", [128, 4, 32], F32)
    sgm = sb("sgm", [128, 512], F32)
    hmtmp = sb("hmtmp", [128, 512], F32)
    hbf = sb("hbf", [128, 512], BF16)
    hbf1 = sb("hbf1", [128, 512], BF16)
    ssq = sb("ssq", [128, 4, 4], F32)
    habf = sb("habf", [128, 4, 128], BF16)
    rr = sb("rr", [128, 8], F32)
    ident = sb("ident", [128, 128], F32)
    identb = sb("identb", [128, 128], BF16)
    umaskf = sb("umaskf", [128, 128], F32)
    umaskb = sb("umaskb", [128, 128], BF16)
    mnegf = sb("mnegf", [128, 128], F32)
    mnegb = sb("mnegb", [128, 128], BF16)
    onesf = sb("onesf", [128, 128], F32)
    alibi = sb("alibi", [128, NAL], F32)
    pp = sb("pp", [128, DEPTH * PPW], F32)
    mng = [sb("mng%d" % l, [128, 512], F32) for l in range(DEPTH)]
    ang = [sb("ang%d" % l, [128, 512], F32) for l in range(DEPTH)]
    bif = [sb("bif%d" % l, [128, 8], F32) for l in range(DEPTH)]
    lamv = sb("lamv", [128, 4, 64], F32)
    lamt = [sb("lamt%d" % l, [128, 4], F32) for l in range(DEPTH)]
    cst = sb("cst", [128, 4], F32)
    print("sbuf bytes remaining/partition:", nc.sbuf_bytes_remaining if hasattr(nc, "sbuf_bytes_remaining") else "?")

    banks = [T(nc.alloc_psum_tensor("bank%d" % i, [128, 512], F32).ap(), "bank%d" % i) for i in range(8)]
    mmrot = Rot(banks[0:4])
    tprot = Rot(banks[4:6])

    csem = S.new_dma_sem()
    dma("sp", csem, ident.ap, ident_d, (), [ident])
    dma("sp", csem, umaskf.ap, umask_d, (), [umaskf])
    dma("sp", csem, mnegf.ap, mneg_d, (), [mnegf])
    dma("sp", csem, alibi.ap, alibi_d, (), [alibi])
    dma("sp", csem, pp.ap, pp_d, (), [pp])
    for l in range(DEPTH):
        dma("sp", csem, mng[l].ap, bc_d["m_norm_g"][l:l + 1, :].partition_broadcast(128), (), [mng[l]])
        dma("sp", csem, ang[l].ap, bc_d["a_norm_g"][l:l + 1, :].partition_broadcast(128), (), [ang[l]])
        dma("sp", csem, bif[l].ap, bc_d["b_if"][l:l + 1, :].partition_broadcast(128), (), [bif[l]])
    cp("dve", identb.ap, ident.ap, [ident], [identb])
    cp("dve", umaskb.ap, umaskf.ap, [umaskf], [umaskb])
    cp("dve", mnegb.ap, mnegf.ap, [mnegf], [mnegb])
    mset("pool", onesf.ap, 1.0, [onesf])
    mset("pool", cst.ap[:, 0:1], EPS_RES, [cst])
    mset("pool", cst.ap[:, 1:2], LN_EPS, [cst])
    mset("pool", cst.ap[:, 2:3], LN_KSCALE, [cst])
    mset("pool", cst.ap[:, 3:4], 1.0, [cst])
    mset("pool", vaug.ap, 1.0, [vaug])
    for l in range(DEPTH):
        mset("pool", Cst[l].ap, 0.0, [Cst[l]])
        mset("pool", Cbf[l].ap, 0.0, [Cbf[l]])
        mset("pool", hist[l].ap, 0.0, [hist[l]])
        mset("pool", metaV[l].ap, 1.0, [metaV[l]])
    for l in range(DEPTH):
        lam_init = 0.8 - 0.6 * math.exp(-0.3 * l)
        for j, n in enumerate(["lam_q1", "lam_k1", "lam_q2", "lam_k2"]):
            dma("sp", csem, lamv.ap[:, j, :], bc_d[n][l:l + 1, :].partition_broadcast(128), (), [lamv])
        s0 = sm.next()
        tt("dve", s0.ap[:, 0:64], lamv.ap[:, 0, :], lamv.ap[:, 1, :], ALU.mult, [lamv], [s0])
        s1 = sm.next()
        S.op("dve", lambda e, s0=s0, s1=s1: e.reduce_sum(s1.ap[:, 0:1], s0.ap[:, 0:64], mybir.AxisListType.X), [s0], [s1])
        tt("dve", s0.ap[:, 0:64], lamv.ap[:, 2, :], lamv.ap[:, 3, :], ALU.mult, [lamv, s1], [s0])
        S.op("dve", lambda e, s0=s0, s1=s1: e.reduce_sum(s1.ap[:, 1:2], s0.ap[:, 0:64], mybir.AxisListType.X), [s0], [s1])
        act(s1.ap[:, 2:4], s1.ap[:, 0:2], AF.Exp, [s1], [s1])
        tt("dve", s1.ap[:, 4:5], s1.ap[:, 3:4], s1.ap[:, 2:3], ALU.subtract, [s1], [s1])
        ts("dve", lamt[l].ap[:, 0:1], s1.ap[:, 4:5], -lam_init, None, ALU.add, None, [s1], [lamt[l]])
        ts("dve", ang[l].ap, ang[l].ap, 1.0 - lam_init, None, ALU.mult, None, [ang[l]], [ang[l]])

    wlane = [S.new_dma_sem() for _ in range(4)]
    npiece = 0
    for l in range(DEPTH):
        for n in WNAMES:
            rows = WSHAPES[n][0]
            step = 128
            for r0 in range(0, rows, step):
                r1 = min(rows, r0 + step)
                ln_ = npiece % 4
                npiece += 1
                dma("pool", wlane[ln_], wbf_d[(l, n)][r0:r1, :], w_d[n][l, r0:r1, :], (), [wbfT[(l, n)][ln_]])

    groups = [dict(meta=True, nt=[NMETA], pos0=0, row0=None, g=-1)]
    for g in range(NG):
        groups.append(dict(meta=False, nt=[128] * 4, pos0=NMETA + GT * g, row0=GT * g, g=g))
    xsem = S.new_dma_sem()
    osem = [S.new_dma_sem() for _ in range(4)]
    ksem = [[S.new_dma_sem() for _ in range(4)] for _ in range(DEPTH)]
    vsem = [[S.new_dma_sem() for _ in range(4)] for _ in range(DEPTH)]

    def load_wA(l, n, c0, ncols=256):
        i = wArot.next()
        src = wbf_d[(l, n)][:, c0:c0 + ncols].rearrange("(kc p) c -> p kc c", p=128)
        dma("sp", wA_sem[i], wA[i].ap[:, :, 0:ncols], src, wbfT[(l, n)], [wA[i]])
        return wA[i]

    def load_wB(l, n, c0, ncols, nkc):
        i = wBrot.next()
        src = wbf_d[(l, n)][:, c0:c0 + ncols].rearrange("(kc p) c -> p kc c", p=128)
        if ncols == 512:
            dst = wB[i].ap[:, 0:nkc, :]
        else:
            dst = wB[i].ap.rearrange("p a b -> p (a b)")[:, 0:nkc * ncols].rearrange("p (a b) -> p a b", a=nkc)
        dma("sp", wB_sem[i], dst, src, wbfT[(l, n)], [wB[i]])
        return wB[i], dst

    def stage_load(G):
        if G["meta"]:
            dma("sp", xsem, h.ap[0:NMETA, 0, :], meta_d, (), hts)
        else:
            src = x_d[G["row0"]:G["row0"] + GT, :].rearrange("(i p) d -> p i d", p=128)
            dma("sp", xsem, h.ap, src, (), hts)

    stg = [(cacc.items[0], cacc.items[1]), (raw.items[0], raw.items[1])]
    stg_sem = [(S.new_dma_sem(), S.new_dma_sem()), (S.new_dma_sem(), S.new_dma_sem())]

    def prefetch_xT(Gn):
        S.label = "load"
        for ti, nt in enumerate(Gn["nt"]):
            r0 = Gn["row0"] + ti * 128
            sl = ti % 2
            for half in range(2):
                tb_ = stg[sl][half]
                dma("sp", stg_sem[sl][half], tb_.ap[0:nt, 0:512], x_d[r0:r0 + nt, half * 512:(half + 1) * 512], (), [tb_])
                pt = tprot.next()
                for c in range(4):
                    tr(pt.ap[:, c * 128:c * 128 + nt], tb_.ap[0:nt, c * 128:(c + 1) * 128], ident.ap[0:nt, 0:nt], [tb_, ident], [pt])
                src = pt.ap.rearrange("p (c t) -> p c t", t=128)[:, :, 0:nt]
                cp("act", xT.ap[:, half * 4:half * 4 + 4, ti * 128:ti * 128 + nt], src, [pt], [xT])

    def stage_ffn(G, l, which, prefetch=None, late_load=None):
        S.label = "ffn_gu"
        NT = sum(G["nt"])
        ntl = G["nt"]
        pre = "ffn%d_" % which
        for blk in range(11):
            wg = load_wA(l, pre + "w_gate", blk * 256)
            wu = load_wA(l, pre + "w_up", blk * 256)
            for cc in range(2):
                ch = blk * 2 + cc
                pg = mmrot.next()
                pu = mmrot.next()
                for kc in range(8):
                    mm(pg.ap[:, 0:NT], wg.ap[:, kc, cc * 128:(cc + 1) * 128], xT.ap[:, kc, 0:NT], kc == 0, kc == 7, [wg, xT], [pg])
                for kc in range(8):
                    mm(pu.ap[:, 0:NT], wu.ap[:, kc, cc * 128:(cc + 1) * 128], xT.ap[:, kc, 0:NT], kc == 0, kc == 7, [wu, xT], [pu])
                sg = ftmp.next()
                act(sg.ap[:, 0:NT], pg.ap[:, 0:NT], AF.Silu, [pg], [sg])
                tt("dve", actT.ap[:, ch, 0:NT], sg.ap[:, 0:NT], pu.ap[:, 0:NT], ALU.mult, [sg, pu], [actT])
        if late_load is not None:
            S.label = "load"
            stage_load(late_load)
        if prefetch is not None:
            prefetch_xT(prefetch)
        S.label = "ffn_down"
        nti = len(ntl)
        nt0 = ntl[0]

        def tail(dch, ys):
            pt = tprot.next()
            for ti, nt in enumerate(ntl):
                tr(pt.ap[0:nt, ti * 128:(ti + 1) * 128], ys.ap[:, ti * 128:ti * 128 + nt], ident.ap, [ys, ident], [pt])
            hv = h.ap[0:nt0, 0:nti, dch * 128:(dch + 1) * 128]
            pv = pt.ap[0:nt0, 0:nti * 128].rearrange("p (i c) -> p i c", c=128)
            stt("dve", hv, pv, C_FFN, hv, ALU.mult, ALU.add, [pt] + hts[0:nti], hts[0:nti])
        pending = None
        for blk in range(4):
            i = wDrot.next()
            src = wbf_d[(l, pre + "w_down")][:, blk * 256:(blk + 1) * 256].rearrange("(fc p) c -> p fc c", p=128)
            dma("sp", wD_sem[i], wD[i].ap, src, wbfT[(l, pre + "w_down")], [wD[i]])
            for cc in range(2):
                dch = blk * 2 + cc
                py = mmrot.next()
                for fc in range(NFF):
                    mm(py.ap[:, 0:NT], wD[i].ap[:, fc, cc * 128:(cc + 1) * 128], actT.ap[:, fc, 0:NT], fc == 0, fc == NFF - 1, [wD[i], actT], [py])
                ys = ftmp.next()
                cp("act", ys.ap[:, 0:NT], py.ap[:, 0:NT], [py], [ys])
                if pending is not None:
                    tail(*pending)
                pending = (dch, ys)
        tail(*pending)

    def stage_ln(G, l, k, final):
        S.label = "ln"
        ntl = G["nt"]
        gname, bname = ["ln1_g", "ln2_g", "ln3_g"][k], ["ln1_b", "ln2_b", "ln3_b"][k]
        pg0 = l * PPW + 56 + (2 * k) * 8
        pb0 = l * PPW + 56 + (2 * k + 1) * 8
        i = lnprot.next()
        dma("sp", lnp_sem[i][0], lnp[i].ap[:, 0, :], bc_d[gname][l:l + 1, :].partition_broadcast(128), (), [lnpg[i]])
        dma("sp", lnp_sem[i][1], lnp[i].ap[:, 1, :], bc_d[bname][l:l + 1, :].partition_broadcast(128), (), [lnpb[i]])
        def front(ti, nt):
            hv = h.ap[0:nt, ti, :]
            S.op("dve", lambda e, nt=nt, ti=ti: e.bn_stats(stats.ap[0:nt, 0, :], h.ap[0:nt, ti, 0:512]), [hts[ti]], [stats])
            S.op("dve", lambda e, nt=nt, ti=ti: e.bn_stats(stats.ap[0:nt, 1, :], h.ap[0:nt, ti, 512:1024]), [hts[ti]], [stats])
            s0 = sm.next()
            S.op("dve", lambda e, nt=nt, s0=s0: e.bn_aggr(s0.ap[0:nt, 0:2], stats.ap[0:nt, 0:2, :]), [stats], [s0])
            act(s0.ap[0:nt, 2:3], s0.ap[0:nt, 1:2], AF.Ln, [s0, cst], [s0], bias=cst.ap[0:nt, 0:1], scale=1.0)
            act(s0.ap[0:nt, 3:4], s0.ap[0:nt, 2:3], AF.Exp, [s0], [s0], scale=-0.5)
            ts("dve", hv, hv, s0.ap[0:nt, 0:1], s0.ap[0:nt, 3:4], ALU.subtract, ALU.mult, [hts[ti], s0], [hts[ti]])

        def back(ti, nt):
            hv = h.ap[0:nt, ti, :]
            if not final:
                for half in range(2):
                    pt = tprot.next()
                    for c in range(4):
                        ch = half * 4 + c
                        tr(pt.ap[:, c * 128:c * 128 + nt], h.ap[0:nt, ti, ch * 128:(ch + 1) * 128], ident.ap[0:nt, 0:nt], [hts[ti], ident], [pt])
                    for c in range(4):
                        ch = half * 4 + c
                        if c != 3:
                            act(xT.ap[:, ch, ti * 128:ti * 128 + nt], pt.ap[:, c * 128:c * 128 + nt], AF.Identity, [pt, pp], [xT],
                                bias=pp.ap[:, pb0 + ch:pb0 + ch + 1], scale=pp.ap[:, pg0 + ch:pg0 + ch + 1])
                        else:
                            ts("dve", xT.ap[:, ch, ti * 128:ti * 128 + nt], pt.ap[:, c * 128:c * 128 + nt],
                               pp.ap[:, pg0 + ch:pg0 + ch + 1], pp.ap[:, pb0 + ch:pb0 + ch + 1], ALU.mult, ALU.add, [pt, pp], [xT])
            tt("pool", hv, hv, lnp[i].ap[0:nt, 0, :], ALU.mult, [hts[ti], lnpg[i]], [hts[ti]])
            tt("pool", hv, hv, lnp[i].ap[0:nt, 1, :], ALU.add, [hts[ti], lnpb[i]], [hts[ti]])
            if final and not G["meta"]:
                r0 = G["row0"] + ti * 128
                dma("pool", osem[ti], out_d[r0:r0 + nt, :], hv, [hts[ti]], [])

        for ti, nt in enumerate(ntl):
            front(ti, nt)
            if ti >= 1:
                back(ti - 1, ntl[ti - 1])
        back(len(ntl) - 1, ntl[-1])

    def stage_proj_feat(G, l):
        S.label = "proj_feat"
        NT = sum(G["nt"])
        ppb = l * PPW
        for blk in range(4):
            w = load_wA(l, "w_in", O_MQ + blk * 256)
            for cc in range(2):
                ch = blk * 2 + cc
                p = mmrot.next()
                for kc in range(8):
                    mm(p.ap[:, 0:NT], w.ap[:, kc, cc * 128:(cc + 1) * 128], xT.ap[:, kc, 0:NT], kc == 0, kc == 7, [w, xT], [p])
                r = raw.next()
                wcol = lambda j: pp.ap[:, ppb + ch * 4 + j:ppb + ch * 4 + j + 1]
                cp("dve", r.ap[:, 0:3], hist[l].ap[:, ch, :], [hist[l]], [r])
                cp("act", r.ap[:, 3:3 + NT], p.ap[:, 0:NT], [p], [r])
                ca = cacc.next()
                act(ca.ap[:, 0:NT], p.ap[:, 0:NT], AF.Identity, [p, pp], [ca], scale=wcol(3))
                for j in range(3):
                    stt("dve", ca.ap[:, 0:NT], r.ap[:, j:j + NT], wcol(j), ca.ap[:, 0:NT], ALU.mult, ALU.add, [r, pp, ca], [ca])
                cp("dve", hist[l].ap[:, ch, :], r.ap[:, NT:NT + 3], [r], [hist[l]])
                act(qkT.ap[:, ch, 0:NT], ca.ap[:, 0:NT], AF.Silu, [ca, pp], [qkT], bias=pp.ap[:, ppb + 32 + ch:ppb + 32 + ch + 1], scale=1.0)
        for blk in range(4):
            w = load_wA(l, "w_in", O_AQ + blk * 256)
            for cc in range(2):
                ch = blk * 2 + cc
                p = mmrot.next()
                for kc in range(8):
                    mm(p.ap[:, 0:NT], w.ap[:, kc, cc * 128:(cc + 1) * 128], xT.ap[:, kc, 0:NT], kc == 0, kc == 7, [w, xT], [p])
                if ch < 4:
                    cp("dve", aqT.ap[:, ch, 0:NT], p.ap[:, 0:NT], [p], [aqT])
                else:
                    cp("act", akT.ap[:, ch - 4, 0:NT], p.ap[:, 0:NT], [p], [akT])
        if G["meta"]:
            cp("pool", metaKT[l].ap, akT.ap[:, :, 0:NMETA], [akT], [metaKT[l]])
        else:
            for hh in range(4):
                dma("pool", ksem[l][hh], kt_d[l][hh, :, G["pos0"]:G["pos0"] + NT], akT.ap[:, hh, 0:NT], [akT], [ktT[l][hh]])

    def stage_mlstm(G, l):
        S.label = "mlstm"
        ntl = G["nt"]
        nti = len(ntl)
        nt0 = ntl[0]
        wmv, wmv_v = load_wB(l, "w_in", O_MV, 512, 8)
        wav, wav_v = load_wB(l, "w_in", O_AV, 512, 8)
        src = wbf_d[(l, "w_in")][:, O_MIF:O_MIF + 8].rearrange("(kc p) c -> p kc c", p=128)
        dma("sp", wmif_sem, wmif.ap, src, wbfT[(l, "w_in")], [wmif])
        bS, bV, bK, bO0, bO1, bC0, bC1, bA = banks[6], banks[0], banks[1], banks[2], banks[3], banks[4], banks[5], banks[7]
        for ti, nt in enumerate(ntl):
            c0 = ti * 128
            for kc in range(8):
                mm(bS.ap[0:nt, ti * 16:ti * 16 + 8], xT.ap[:, kc, c0:c0 + nt], wmif.ap[:, kc, :], kc == 0, kc == 7, [xT, wmif], [bS])
        bS3 = bS.ap[0:nt0, 0:nti * 16].rearrange("p (t c) -> p t c", c=16)
        bS3f = bS.ap[:, 0:nti * 16].rearrange("p (t c) -> p t c", c=16)
        g3 = gsm.ap[0:nt0, 0:nti, :]
        tt("dve", g3[:, :, 0:8], bS3[:, :, 0:8], bif[l].ap[0:nt0, :].unsqueeze(1).to_broadcast([nt0, nti, 8]), ALU.add, [bS, bif[l]], [gsm])
        act(g3[:, :, 8:12], g3[:, :, 4:8], AF.Exp, [gsm], [gsm], scale=-1.0)
        act(g3[:, :, 12:16], g3[:, :, 8:12], AF.Ln, [gsm, cst], [gsm], bias=cst.ap[0:nt0, 3:4], scale=1.0)
        S.label = "mlstm_f32"
        for ti, nt in enumerate(ntl):
            mm(bS.ap[0:nt, ti * 16 + 8:ti * 16 + 12], umaskf.ap[0:nt, 0:nt], gsm.ap[0:nt, ti, 12:16], True, True, [umaskf, gsm], [bS])
            mm(bS.ap[:, ti * 16 + 12:ti * 16 + 16], onesf.ap[0:nt, :], gsm.ap[0:nt, ti, 12:16], True, True, [onesf, gsm], [bS])
        S.label = "mlstm"
        tt("dve", g3[:, :, 16:20], g3[:, :, 0:4], bS3[:, :, 8:12], ALU.add, [gsm, bS], [gsm])
        act(g3[:, :, 20:24], g3[:, :, 16:20], AF.Exp, [gsm, cst], [gsm], bias=cst.ap[0:nt0, 2:3], scale=1.0)
        act(g3[:, :, 24:28], bS3[:, :, 8:12], AF.Exp, [bS], [gsm])
        act(gsm.ap[:, 0:nti, 28:32], bS3f[:, :, 12:16], AF.Exp, [bS], [gsm], scale=-1.0)
        for ti, nt in enumerate(ntl):
            c0 = ti * 128
            ew = gsm.ap[0:nt, ti, 20:24]
            emb = gsm.ap[0:nt, ti, 24:28]
            tt("dve", ctmp.ap, Cst[l].ap, gsm.ap[:, ti, 28:32].unsqueeze(2).to_broadcast([128, 4, 129]), ALU.mult, [Cst[l], gsm], [ctmp])
            for kc in range(8):
                mm(bA.ap[0:nt, :], xT.ap[:, kc, c0:c0 + nt], wav_v[:, kc, :], kc == 0, kc == 7, [xT, wav], [bA])
            cp("act", vaug.ap[0:nt, ti, :, 0:128], bA.ap[0:nt, :].rearrange("p (h c) -> p h c", c=128), [bA], [vaug])
            if G["meta"]:
                cp("pool", metaV[l].ap[0:nt, :, 0:128], vaug.ap[0:nt, ti, :, 0:128], [vaug], [metaV[l]])
            else:
                p0 = G["pos0"] + c0
                dma("pool", vsem[l][ti], v_d[l][:, p0:p0 + nt, :].rearrange("h t c -> t h c"), vaug.ap[0:nt, ti, :, :], [vaug], [vT[l][ti]])
            for kc in range(8):
                mm(bV.ap[0:nt, :], xT.ap[:, kc, c0:c0 + nt], wmv_v[:, kc, :], kc == 0, kc == 7, [xT, wmv], [bV])
            tt("dve", vw.ap[0:nt, :, 0:128], bV.ap[0:nt, :].rearrange("p (h c) -> p h c", c=128),
               ew.unsqueeze(2).to_broadcast([nt, 4, 128]), ALU.mult, [bV, gsm], [vw])
            cp("dve", vw.ap[0:nt, :, 128:129], ew.unsqueeze(2), [gsm], [vw])
            bKb = bK.ap.bitcast(BF16)
            for hh in range(4):
                tr(bKb[0:nt, hh * 128:(hh + 1) * 128], qkT.ap[:, 4 + hh, c0:c0 + nt], identb.ap, [qkT, identb], [bK])
            cp("act", ktok.ap[0:nt, :], bKb[0:nt, 0:512], [bK], [ktok])
            for hh in range(4):
                mm(banks[7].ap[0:nt, hh * 128:hh * 128 + nt], qkT.ap[:, 4 + hh, c0:c0 + nt], qkT.ap[:, hh, c0:c0 + nt], True, True, [qkT], [banks[7]])
            tt("dve", SM.ap[0:nt, :, 0:nt], banks[7].ap[0:nt, :].rearrange("p (h c) -> p h c", c=128)[:, :, 0:nt],
               umaskb.ap[0:nt, 0:nt].unsqueeze(1).to_broadcast([nt, 4, nt]), ALU.mult, [banks[7], umaskb], [SM])
            for hh in range(4):
                bo = bO0 if hh < 2 else bO1
                o0 = (hh % 2) * 129
                mm(bo.ap[0:nt, o0:o0 + 129], SM.ap[0:nt, hh, 0:nt], vw.ap[0:nt, hh, :], True, False, [SM, vw], [bo])
                mm(bo.ap[0:nt, o0:o0 + 129], qkT.ap[:, hh, c0:c0 + nt], Cbf[l].ap[:, hh, :], False, True, [qkT, Cbf[l]], [bo])
            for hh in range(4):
                bc = bC0 if hh < 2 else bC1
                o0 = (hh % 2) * 129
                mm(bc.ap[:, o0:o0 + 129], ktok.ap[0:nt, hh * 128:(hh + 1) * 128], vw.ap[0:nt, hh, :], True, True, [ktok, vw], [bc])
            for hh in range(4):
                bc = bC0 if hh < 2 else bC1
                o0 = (hh % 2) * 129
                stt("dve", Cst[l].ap[:, hh, :], bc.ap[:, o0:o0 + 129], gsm.ap[:, ti, 28 + hh:29 + hh], ctmp.ap[:, hh, :], ALU.mult, ALU.add, [bc, gsm, ctmp], [Cst[l]])
            cp("act", Cbf[l].ap, Cst[l].ap, [Cst[l]], [Cbf[l]])
            s1 = sm.next()
            for half, bo in enumerate((bO0, bO1)):
                cp("dve", s1.ap[0:nt, half * 2:half * 2 + 2], bo.ap[0:nt, 0:258].rearrange("p (h c) -> p h c", c=129)[:, :, 128], [bo], [s1])
            stt("dve", s1.ap[0:nt, 4:8], s1.ap[0:nt, 0:4], -1.0, s1.ap[0:nt, 0:4], ALU.mult, ALU.max, [s1], [s1])
            tt("dve", s1.ap[0:nt, 8:12], s1.ap[0:nt, 4:8], emb, ALU.max, [s1, gsm], [s1])
            tt("dve", s1.ap[0:nt, 12:16], s1.ap[0:nt, 8:12], s1.ap[0:nt, 8:12], ALU.mult, [s1], [s1])
            for hh in range(4):
                bo = bO0 if hh < 2 else bO1
                o0 = (hh % 2) * 129
                S.op("dve", lambda e, nt=nt, hh=hh, bo=bo, o0=o0: e.bn_stats(stats.ap[0:nt, hh, :], bo.ap[0:nt, o0:o0 + 128]), [bo], [stats])
            for hh in range(4):
                S.op("dve", lambda e, nt=nt, hh=hh, s1=s1: e.bn_aggr(s1.ap[0:nt, 32 + 2 * hh:34 + 2 * hh], stats.ap[0:nt, hh:hh + 1, :]), [stats], [s1])
            varv = s1.ap[0:nt, 32:40].rearrange("p (h c) -> p h c", c=2)[:, :, 1]
            stt("dve", s1.ap[0:nt, 16:20], s1.ap[0:nt, 12:16], LN_EPS, varv, ALU.mult, ALU.add, [s1], [s1])
            act(s1.ap[0:nt, 20:24], s1.ap[0:nt, 16:20], AF.Ln, [s1], [s1])
            act(s1.ap[0:nt, 24:28], s1.ap[0:nt, 20:24], AF.Exp, [s1], [s1], scale=-0.5)
            for hh in range(4):
                bo = bO0 if hh < 2 else bO1
                o0 = (hh % 2) * 129
                ts("dve", hn.ap[0:nt, ti, hh * 128:(hh + 1) * 128], bo.ap[0:nt, o0:o0 + 128],
                   s1.ap[0:nt, 32 + 2 * hh:33 + 2 * hh], s1.ap[0:nt, 24 + hh:25 + hh], ALU.subtract, ALU.mult, [bo, s1], [hn])

    def stage_out_m(G, l):
        S.label = "out_a"
        ntl = G["nt"]
        wmo, wmo_v = load_wB(l, "w_in", O_MO, 512, 8)
        hb2 = [hbf, hbf1]

        def back(ti, nt, hb):
            c0 = ti * 128
            pt = tprot.next()
            ptb = pt.ap.bitcast(BF16)
            for fc in range(4):
                tr(ptb[:, fc * 128:fc * 128 + nt], hb.ap[0:nt, fc * 128:(fc + 1) * 128], identb.ap[0:nt, 0:nt], [hb, identb], [pt])
            cp("act", hmT.ap[:, :, c0:c0 + nt], ptb[:, 0:512].rearrange("p (c t) -> p c t", t=128)[:, :, 0:nt], [pt], [hmT])
        pend = None
        for ti, nt in enumerate(ntl):
            c0 = ti * 128
            hb = hb2[ti % 2]
            pm = mmrot.next()
            for kc in range(8):
                mm(pm.ap[0:nt, :], xT.ap[:, kc, c0:c0 + nt], wmo_v[:, kc, :], kc == 0, kc == 7, [xT, wmo], [pm])
            act(sgm.ap[0:nt, :], pm.ap[0:nt, :], AF.Sigmoid, [pm], [sgm])
            tt("dve", hmtmp.ap[0:nt, :], hn.ap[0:nt, ti, :], sgm.ap[0:nt, :], ALU.mult, [hn, sgm], [hmtmp])
            tt("dve", hb.ap[0:nt, :], hmtmp.ap[0:nt, :], mng[l].ap[0:nt, :], ALU.mult, [hmtmp, mng[l]], [hb])
            if pend is not None:
                back(*pend)
            pend = (ti, nt, hb)
        back(*pend)

    attn_tail_cb = [None]

    def stage_attn(G, l):
        S.label = "attn"
        ntl = G["nt"]
        NT = sum(ntl)
        nti = len(ntl)
        nt0_ = ntl[0]
        pos0 = G["pos0"]
        g = G["g"]
        sbanks = Rot([(banks[0], banks[1]), (banks[2], banks[3])])
        def accv(qt, c):
            a_ = qt * 2 + c
            return banks[4 + a_ // 3], (a_ % 3) * 129
        tpb = banks[7]
        mset("dve", ssq.ap, 0.0, [ssq])
        deferred = [None]

        pend_fin = [None]

        def finalize(hh, cpy):
            S.label = "attn_fin"

            def accs(qt, c):
                a_ = qt * 2 + c
                return cpy[a_ // 3], (a_ % 3) * 129
            for qt in range(nti):
                ntq = ntl[qt]
                ab0, c0_ = accs(qt, 0)
                ab1, c1_ = accs(qt, 1)
                S.op("dve", lambda e, ntq=ntq, ab0=ab0, c0_=c0_: e.reciprocal(rr.ap[0:ntq, 0:1], ab0.ap[0:ntq, c0_ + 128:c0_ + 129]), [ab0], [rr])
                S.op("dve", lambda e, ntq=ntq, ab1=ab1, c1_=c1_: e.reciprocal(rr.ap[0:ntq, 1:2], ab1.ap[0:ntq, c1_ + 128:c1_ + 129]), [ab1], [rr])
                tt("dve", rr.ap[0:ntq, 2:3], rr.ap[0:ntq, 1:2], lamt[l].ap[0:ntq, 0:1], ALU.mult, [rr, lamt[l]], [rr])
                ov = ao.ap[0:ntq, qt, hh * 128:(hh + 1) * 128]
                ts("dve", ov, ab0.ap[0:ntq, c0_:c0_ + 128], rr.ap[0:ntq, 0:1], None, ALU.mult, None, [ab0, rr], [ao])
                stt("dve", ov, ab1.ap[0:ntq, c1_:c1_ + 128], rr.ap[0:ntq, 2:3], ov, ALU.mult, ALU.add, [ab1, rr, ao], [ao])
                act(hmtmp.ap[0:ntq, 0:128], ov, AF.Square, [ao], [hmtmp, ssq], accum=ssq.ap[0:ntq, qt, hh:hh + 1])

        def head_tail1(hh):
            S.label = "attn_fin"
            for qt in range(nti):
                ntq = ntl[qt]
                s0 = sm.next()
                act(s0.ap[0:ntq, 0:1], ssq.ap[0:ntq, qt, hh:hh + 1], AF.Ln, [ssq, cst], [s0], bias=cst.ap[0:ntq, 1:2], scale=1.0 / 128.0)
                act(s0.ap[0:ntq, 1:2], s0.ap[0:ntq, 0:1], AF.Exp, [s0], [s0], scale=-0.5)
                stt("dve", habf.ap[0:ntq, qt, :], ao.ap[0:ntq, qt, hh * 128:(hh + 1) * 128], s0.ap[0:ntq, 1:2],
                    ang[l].ap[0:ntq, hh * 128:(hh + 1) * 128], ALU.mult, ALU.mult, [ao, s0, ang[l]], [habf])

        def head_tail2(hh):
            S.label = "attn_tail"
            tb = tpb.ap.bitcast(BF16)
            for qt in range(nti):
                ntq = ntl[qt]
                tr(tb[:, qt * 128:qt * 128 + ntq], habf.ap[0:ntq, qt, :], identb.ap[0:ntq, 0:ntq], [habf, identb], [tpb])
            cp("act", haT.ap[:, hh, 0:NT], tb[:, 0:NT], [tpb], [haT])

        for hh in range(4):
            slope = SLOPES[hh]
            bank_started = {}
            started = [[False, False] for _ in range(nti)]
            units = []
            if not G["meta"]:
                if slope * (pos0 - NMETA + 1) <= WIN_CUT:
                    colB = hh * (NA_M + 17) + NA_M + g
                    units.append(("sb", metaKT[l].ap[:, hh, :], metaV[l].ap[0:NMETA, hh, :], NMETA, colB, 0, [metaKT[l], metaV[l]], None))
                nprev = 4 * g
                u_lo = 0
                while u_lo < nprev and slope * (pos0 - (NMETA + 128 * (u_lo + 1)) + 1) > WIN_CUT:
                    u_lo += 1
                u = u_lo
                while u < nprev:
                    n_u = min(16, nprev - u)
                    for j in range(n_u):
                        m = nprev - (u + j)
                        colA = hh * (NA_M + 17) + (m + 3)
                        units.append(("dram", u, n_u, j, colA))
                    u += n_u
            for kt in range(nti):
                nk = ntl[kt]
                if G["meta"]:
                    colA = hh * (NA_M + 17) + NA_M + 16
                else:
                    colA = hh * (NA_M + 17) + (-kt + 3)
                units.append(("sb", akT.ap[:, hh, kt * 128:kt * 128 + nk], vaug.ap[0:nk, kt, hh, :], nk, colA, kt * 128, [akT, vaug], kt))

            cur_chunk = [None]

            def resolve(un):
                if un[0] == "sb":
                    return un[1:]
                _, u0, n_u, j, colA = un
                if cur_chunk[0] is None or cur_chunk[0][0] != u0:
                    bi = KVrot.next()
                    p_lo = NMETA + 128 * u0
                    dma("sp", KV_sem[bi][0], KTc[bi].ap[:, 0:128 * n_u], kt_d[l][hh, :, p_lo:p_lo + 128 * n_u], [ktT[l][hh]], [KTc[bi]])
                    dma("sp", KV_sem[bi][1], Vc[bi].ap[:, 0:n_u, :], v_d[l][hh, p_lo:p_lo + 128 * n_u, :].rearrange("(u p) c -> p u c", p=128), vT[l], [Vc[bi]])
                    cur_chunk[0] = (u0, bi)
                bi = cur_chunk[0][1]
                return (KTc[bi].ap[:, j * 128:(j + 1) * 128], Vc[bi].ap[:, j, :], 128, colA, 0, [KTc[bi], Vc[bi]], None)

            def phaseA(un):
                S.label = "attn_S"
                kT_ap, v_ap, nk, colA, q0, kdeps, diag_kt = resolve(un)
                sp = sbanks.next()
                pts = []
                for c in range(2):
                    bk = sp[c]
                    rows = slice(64 * c, 64 * c + 64)
                    if diag_kt is None:
                        mm(bk.ap[0:nk, q0:NT], kT_ap[rows, :], aqT.ap[rows, hh, q0:NT], True, True, kdeps + [aqT], [bk])
                    else:
                        ntq = ntl[diag_kt]
                        mm(bk.ap[0:nk, q0:q0 + ntq], kT_ap[rows, :], aqT.ap[rows, hh, q0:q0 + ntq], True, False, kdeps + [aqT], [bk])
                        mm(bk.ap[0:nk, q0:q0 + ntq], identb.ap[0:nk, 0:nk], mnegb.ap[0:nk, 0:ntq], False, True, [identb, mnegb], [bk])
                        if q0 + 128 < NT:
                            mm(bk.ap[0:nk, q0 + 128:NT], kT_ap[rows, :], aqT.ap[rows, hh, q0 + 128:NT], True, True, kdeps + [aqT], [bk])
                    pt = PT.next()
                    act(pt.ap[0:nk, q0:NT], bk.ap[0:nk, q0:NT], AF.Exp, [bk, alibi], [pt], bias=alibi.ap[0:nk, colA:colA + 1], scale=0.125)
                    pts.append(pt)
                return (pts, v_ap, nk, q0, kdeps, diag_kt)

            def phaseB(st):
                S.label = "attn_PV"
                pts, v_ap, nk, q0, kdeps, diag_kt = st
                for c in range(2):
                    pt = pts[c]
                    for qt in range(q0 // 128, nti):
                        ntq = ntl[qt]
                        last = (diag_kt is not None and qt == diag_kt)
                        ab, col = accv(qt, c)
                        first = ab.name not in bank_started
                        bank_started[ab.name] = True
                        S.op("pe", lambda e, o_=ab.ap[0:ntq, col:col + 129], l_=pt.ap[0:nk, qt * 128:qt * 128 + ntq], r_=v_ap, f_=first, s_=last:
                             e.matmul(o_, l_, r_, start=f_, stop=s_, skip_group_check=True), kdeps + [pt], [ab])

            prev = None
            fin_at = min(1, len(units) - 1)
            defer_at = min(4, len(units) - 1)
            for ui, un in enumerate(units):
                st = phaseA(un)
                if ui == fin_at and pend_fin[0] is not None:
                    hh_p, cpy_p = pend_fin[0]
                    pend_fin[0] = None
                    finalize(hh_p, cpy_p)
                    head_tail1(hh_p)
                    deferred[0] = hh_p
                if ui == defer_at and deferred[0] is not None:
                    head_tail2(deferred[0])
                    deferred[0] = None
                if prev is not None:
                    phaseB(prev)
                prev = st
            phaseB(prev)
            S.label = "attn_fin"
            nacc = 2 * nti
            cpy = []
            for bi_ in range((nacc + 2) // 3):
                nsl = min(3, nacc - 3 * bi_)
                fb = ftmp.next()
                cp("act" if bi_ % 2 == 0 else "dve", fb.ap[0:nt0_, 0:129 * nsl], banks[4 + bi_].ap[0:nt0_, 0:129 * nsl], [banks[4 + bi_]], [fb])
                cpy.append(fb)
            pend_fin[0] = (hh, cpy)
        hh_l, cpy_l = pend_fin[0]
        pend_fin[0] = None
        finalize(hh_l, cpy_l)
        head_tail1(hh_l)
        attn_tail_cb[0] = (lambda hh_=hh_l: head_tail2(hh_))

    def stage_out(G, l):
        ntl = G["nt"]
        NT = sum(ntl)
        ppb = l * PPW
        S.label = "out_b"
        wbm, wbm_v = load_wB(l, "w_bm", 0, 1024, 4)
        wba, wba_v = load_wB(l, "w_ba", 0, 1024, 4)
        for j in range(4):
            wgm = load_wA(l, "w_in", O_G + j * 256)
            wga = load_wA(l, "w_in", O_G + 1024 + j * 256)
            for cc in range(2):
                c = 2 * j + cc
                pgm, pga, pym, pya = mmrot.next(), mmrot.next(), mmrot.next(), mmrot.next()
                for kc in range(8):
                    mm(pgm.ap[:, 0:NT], wgm.ap[:, kc, cc * 128:(cc + 1) * 128], xT.ap[:, kc, 0:NT], kc == 0, kc == 7, [wgm, xT], [pgm])
                for kc in range(8):
                    mm(pga.ap[:, 0:NT], wga.ap[:, kc, cc * 128:(cc + 1) * 128], xT.ap[:, kc, 0:NT], kc == 0, kc == 7, [wga, xT], [pga])
                for fc in range(4):
                    mm(pym.ap[:, 0:NT], wbm_v[:, fc, c * 128:(c + 1) * 128], hmT.ap[:, fc, 0:NT], fc == 0, fc == 3, [wbm, hmT], [pym])
                if attn_tail_cb[0] is not None:
                    attn_tail_cb[0]()
                    attn_tail_cb[0] = None
                    S.label = "out_b"
                for fc in range(4):
                    mm(pya.ap[:, 0:NT], wba_v[:, fc, c * 128:(c + 1) * 128], haT.ap[:, fc, 0:NT], fc == 0, fc == 3, [wba, haT], [pya])
                s_m = ftmp.next()
                s_a = ftmp.next()
                act(s_m.ap[:, 0:NT], pgm.ap[:, 0:NT], AF.Sigmoid, [pgm, pp], [s_m], bias=pp.ap[:, ppb + 40 + c:ppb + 41 + c], scale=1.0)
                act(s_a.ap[:, 0:NT], pga.ap[:, 0:NT], AF.Sigmoid, [pga, pp], [s_a], bias=pp.ap[:, ppb + 48 + c:ppb + 49 + c], scale=1.0)
                tt("dve", s_m.ap[:, 0:NT], s_m.ap[:, 0:NT], pym.ap[:, 0:NT], ALU.mult, [s_m, pym], [s_m])
                tt("dve", s_a.ap[:, 0:NT], s_a.ap[:, 0:NT], pya.ap[:, 0:NT], ALU.mult, [s_a, pya], [s_a])
                tt("dve", zT.ap[:, c, 0:NT], s_m.ap[:, 0:NT], s_a.ap[:, 0:NT], ALU.add, [s_m, s_a], [zT])
        S.label = "out_c"
        wo = []
        for half in range(2):
            wo.append(load_wB(l, "w_out", half * 512, 512, 8))
        for ti, nt in enumerate(ntl):
            c0 = ti * 128
            for half in range(2):
                pm = mmrot.next()
                for kc in range(8):
                    mm(pm.ap[0:nt, :], zT.ap[:, kc, c0:c0 + nt], wo[half][1][:, kc, :], kc == 0, kc == 7, [zT, wo[half][0]], [pm])
                hv = h.ap[0:nt, ti, half * 512:(half + 1) * 512]
                stt("dve", hv, pm.ap[0:nt, :], C_MIX, hv, ALU.mult, ALU.add, [pm, hts[ti]], [hts[ti]])

    def initial_xT(G):
        ntl = G["nt"]
        for ti, nt in enumerate(ntl):
            for half in range(2):
                pt = tprot.next()
                for c in range(4):
                    ch = half * 4 + c
                    tr(pt.ap[:, c * 128:c * 128 + nt], h.ap[0:nt, ti, ch * 128:(ch + 1) * 128], ident.ap[0:nt, 0:nt], [hts[ti], ident], [pt])
                src = pt.ap.rearrange("p (c t) -> p c t", t=128)[:, :, 0:nt]
                cp("act", xT.ap[:, half * 4:half * 4 + 4, ti * 128:ti * 128 + nt], src, [pt], [xT])

    for gi, G in enumerate(groups):
        S.gp = "g%d" % G["g"]
        S.label = "load"
        if gi == 0:
            stage_load(G)
            initial_xT(G)
        Gn = groups[gi + 1] if gi + 1 < len(groups) else None
        for l in range(DEPTH):
            stage_ffn(G, l, 1, late_load=(G if (l == 0 and gi > 0) else None))
            stage_ln(G, l, 0, False)
            stage_proj_feat(G, l)
            stage_mlstm(G, l)
            stage_out_m(G, l)
            stage_attn(G, l)
            stage_out(G, l)
            stage_ln(G, l, 1, False)
            stage_ffn(G, l, 2, prefetch=(Gn if l == DEPTH - 1 else None))
            stage_ln(G, l, 2, final=(l == DEPTH - 1))
    S.wait_all("pool", osem)
    S.wait_all("sp", osem)
    print("instr counts:", {k: v for k, v in S.cnt.items() if k in Sched.ENGS}, "sems:", len(S.sems))
    S.emit()
    nc._labels = S.labels
    return nc


def make_in_maps(inputs, NG, ncores):
    consts = host_consts()
    x = np.asarray(inputs["x"], dtype=np.float32)
    pp = np.zeros((128, DEPTH, PPW), dtype=np.float32)
    cw = np.asarray(inputs["conv_w"], dtype=np.float32)
    cb = np.asarray(inputs["conv_b"], dtype=np.float32)
    bg = np.asarray(inputs["b_gate"], dtype=np.float32)
    pp[:, :, 0:32] = cw.reshape(DEPTH, 4, 8, 128).transpose(3, 0, 2, 1).reshape(128, DEPTH, 32)
    pp[:, :, 32:40] = cb.reshape(DEPTH, 8, 128).transpose(2, 0, 1)
    pp[:, :, 40:56] = bg.reshape(DEPTH, 16, 128).transpose(2, 0, 1)
    for k_, n_ in enumerate(["ln1_g", "ln1_b", "ln2_g", "ln2_b", "ln3_g", "ln3_b"]):
        v_ = np.asarray(inputs[n_], dtype=np.float32)
        pp[:, :, 56 + k_ * 8:56 + (k_ + 1) * 8] = v_.reshape(DEPTH, 8, 128).transpose(2, 0, 1)
    pp = np.ascontiguousarray(pp.reshape(128, DEPTH * PPW))
    shared = {n: np.ascontiguousarray(np.asarray(inputs[n], dtype=np.float32)) for n in WNAMES}
    for n in BC_NAMES:
        shared[n] = np.ascontiguousarray(np.asarray(inputs[n], dtype=np.float32))
    shared["meta"] = np.ascontiguousarray(np.asarray(inputs["meta"], dtype=np.float32))
    shared["pp"] = pp
    shared.update(consts)
    maps = []
    for c in range(ncores):
        m = dict(shared)
        m["x"] = np.ascontiguousarray(x[c, :GT * NG, :])
        maps.append(m)
    return maps


_NC_CACHE = {}


def kernel(**inputs):
    NG = 16
    ncores = 8
    if NG not in _NC_CACHE:
        _NC_CACHE[NG] = build_nc(NG)
    nc = _NC_CACHE[NG]
    maps = make_in_maps(inputs, NG, ncores)
    res = run_bass_kernel_spmd(nc, maps, core_ids=list(range(ncores)))
    out = np.stack([np.asarray(r["out"], dtype=np.float32) for r in res.results], axis=0)
    return out
```
